# Optimizing a Trainium2 kernel written in Bass

```python
import math
import jax
import jax.numpy as jnp
from jax import lax
import numpy as np

D_MODEL = 2048
BATCH = 4
SEQ = 2048
DEPTH = 4
DEC_BATCH = 128
DEC_SEQ = 1
PAST_LEN = 16384
PAGE_SIZE = 128

PLE_DIM = 256
N_MIXERS = 2
N_CONV_LAYERS = (DEPTH + N_MIXERS - 1) // N_MIXERS
N_LRU_LAYERS = DEPTH // N_MIXERS
SC_WIDTH = 3
D_RNN = D_MODEL
LRU_BLOCKS = 8
LRU_BW = D_RNN // LRU_BLOCKS
RG_CONV_WIDTH = 4
LRU_C = 8.0
D_FF = -(-8 * D_MODEL // (3 * 256)) * 256
EPS = 1e-6

kernel_name = "hybrid_shortconv_rglru_decoder_step"


def rmsnorm(x, g):
    xf = x.astype(jnp.float32)
    y = xf * lax.rsqrt(jnp.mean(xf * xf, axis=-1, keepdims=True) + EPS)
    return (y * g.astype(jnp.float32)).astype(x.dtype)


def causal_dwconv(u, buf, w, b=None):
    width = w.shape[0]
    t = u.shape[1]
    full = jnp.concatenate([buf.astype(u.dtype), u], axis=1)
    y = full[:, 0:t] * w[0]
    for k in range(1, width):
        y = y + full[:, k:k + t] * w[k]
    if b is not None:
        y = y + b
    return y, full[:, t:]


def short_conv_mixer(h, buf, w_in, w_conv, w_out):
    bcx = h @ w_in
    b_gate, c_gate, xin = jnp.split(bcx, 3, axis=-1)
    conv, new_buf = causal_dwconv(c_gate * xin, buf, w_conv)
    return (b_gate * conv) @ w_out, new_buf


def _lin_comb(c1, c2):
    a1, b1 = c1
    a2, b2 = c2
    return a1 * a2, a2 * b1 + b2


def rglru_mixer(h, conv_buf, h0, w_x, w_gate, conv_w, conv_b, w_a, b_a, w_i, b_i, lam, w_out):
    gate = jax.nn.gelu(h @ w_gate, approximate=True)
    u, new_conv = causal_dwconv(h @ w_x, conv_buf, conv_w, conv_b)
    bsz, t, _ = u.shape
    ub = u.reshape(bsz, t, LRU_BLOCKS, LRU_BW)
    r = jax.nn.sigmoid(jnp.einsum('btnk,nkj->btnj', ub, w_a).reshape(bsz, t, D_RNN) + b_a)
    i = jax.nn.sigmoid(jnp.einsum('btnk,nkj->btnj', ub, w_i).reshape(bsz, t, D_RNN) + b_i)
    log_a = -LRU_C * r.astype(jnp.float32) * jax.nn.softplus(-lam.astype(jnp.float32))
    a = jnp.exp(log_a)
    mult = jnp.sqrt(-jnp.expm1(2.0 * log_a))
    bx = mult * i.astype(jnp.float32) * u.astype(jnp.float32)
    a_cum, b_cum = lax.associative_scan(_lin_comb, (a, bx), axis=1)
    hs = b_cum + a_cum * h0.astype(jnp.float32)[:, None, :]
    y = (gate * hs.astype(h.dtype)) @ w_out
    return y, new_conv, hs[:, -1].astype(h0.dtype)


def trunk(x, p, conv_state, rgc_state, rgh_state,
          mix_norm, ffn_norm, ple_norm, final_norm,
          sc_w_in, sc_w_conv, sc_w_out,
          rg_w_x, rg_w_gate, rg_conv_w, rg_conv_b, rg_w_a, rg_b_a, rg_w_i, rg_b_i, rg_lambda, rg_w_out,
          ffn_w_gate, ffn_w_up, ffn_w_down, ple_w_gate, ple_w_proj):
    new_conv, new_rgc, new_rgh = [], [], []
    for layer in range(DEPTH):
        j = layer // N_MIXERS
        hn = rmsnorm(x, mix_norm[layer])
        if layer % N_MIXERS == 0:
            y, nb = short_conv_mixer(hn, conv_state[j], sc_w_in[j], sc_w_conv[j], sc_w_out[j])
            new_conv.append(nb)
        else:
            y, nc, nh = rglru_mixer(hn, rgc_state[j], rgh_state[j], rg_w_x[j], rg_w_gate[j],
                                    rg_conv_w[j], rg_conv_b[j], rg_w_a[j], rg_b_a[j],
                                    rg_w_i[j], rg_b_i[j], rg_lambda[j], rg_w_out[j])
            new_rgc.append(nc)
            new_rgh.append(nh)
        x = x + y
        hn = rmsnorm(x, ffn_norm[layer])
        x = x + (jax.nn.silu(hn @ ffn_w_gate[layer]) * (hn @ ffn_w_up[layer])) @ ffn_w_down[layer]
        g = jax.nn.sigmoid(rmsnorm(x, ple_norm[layer]) @ ple_w_gate[layer])
        x = x + g * (p[layer] @ ple_w_proj[layer])
    return rmsnorm(x, final_norm), jnp.stack(new_conv), jnp.stack(new_rgc), jnp.stack(new_rgh)


def setup_inputs(seed: int = 0) -> dict:
    key = jax.random.key(seed)
    ks = iter(jax.random.split(key, 40))
    f32 = jnp.float32

    def nrm(shape, scale=1.0):
        return jax.random.normal(next(ks), shape, f32) * scale

    def gain(shape):
        return 1.0 + 0.05 * jax.random.normal(next(ks), shape, f32)

    a_init = jax.random.uniform(next(ks), (N_LRU_LAYERS, D_RNN), f32, 0.9, 0.999)
    return {
        "x_prompt": nrm((BATCH, SEQ, D_MODEL)),
        "x_sample": nrm((DEC_BATCH, DEC_SEQ, D_MODEL)),
        "p_prompt": nrm((DEPTH, BATCH, SEQ, PLE_DIM)),
        "p_sample": nrm((DEPTH, DEC_BATCH, DEC_SEQ, PLE_DIM)),
        "state_conv": nrm((N_CONV_LAYERS, DEC_BATCH, SC_WIDTH - 1, D_MODEL)),
        "state_rg_conv": nrm((N_LRU_LAYERS, DEC_BATCH, RG_CONV_WIDTH - 1, D_RNN)),
        "state_rg_h": nrm((N_LRU_LAYERS, DEC_BATCH, D_RNN), 0.5),
        "mix_norm": gain((DEPTH, D_MODEL)),
        "ffn_norm": gain((DEPTH, D_MODEL)),
        "ple_norm": gain((DEPTH, D_MODEL)),
        "final_norm": gain((D_MODEL,)),
        "sc_w_in": nrm((N_CONV_LAYERS, D_MODEL, 3 * D_MODEL), D_MODEL ** -0.5),
        "sc_w_conv": nrm((N_CONV_LAYERS, SC_WIDTH, D_MODEL), SC_WIDTH ** -0.5),
        "sc_w_out": nrm((N_CONV_LAYERS, D_MODEL, D_MODEL), D_MODEL ** -0.5),
        "rg_w_x": nrm((N_LRU_LAYERS, D_MODEL, D_RNN), D_MODEL ** -0.5),
        "rg_w_gate": nrm((N_LRU_LAYERS, D_MODEL, D_RNN), D_MODEL ** -0.5),
        "rg_conv_w": nrm((N_LRU_LAYERS, RG_CONV_WIDTH, D_RNN), RG_CONV_WIDTH ** -0.5),
        "rg_conv_b": nrm((N_LRU_LAYERS, D_RNN), 0.01),
        "rg_w_a": nrm((N_LRU_LAYERS, LRU_BLOCKS, LRU_BW, LRU_BW), LRU_BW ** -0.5),
        "rg_b_a": nrm((N_LRU_LAYERS, D_RNN), 0.01),
        "rg_w_i": nrm((N_LRU_LAYERS, LRU_BLOCKS, LRU_BW, LRU_BW), LRU_BW ** -0.5),
        "rg_b_i": nrm((N_LRU_LAYERS, D_RNN), 0.01),
        "rg_lambda": jnp.log(a_init / (1.0 - a_init)),
        "rg_w_out": nrm((N_LRU_LAYERS, D_RNN, D_MODEL), D_RNN ** -0.5),
        "ffn_w_gate": nrm((DEPTH, D_MODEL, D_FF), D_MODEL ** -0.5),
        "ffn_w_up": nrm((DEPTH, D_MODEL, D_FF), D_MODEL ** -0.5),
        "ffn_w_down": nrm((DEPTH, D_FF, D_MODEL), D_FF ** -0.5),
        "ple_w_gate": nrm((DEPTH, D_MODEL, D_MODEL), D_MODEL ** -0.5),
        "ple_w_proj": nrm((DEPTH, PLE_DIM, D_MODEL), PLE_DIM ** -0.5),
    }


def reference(x_prompt, x_sample, p_prompt, p_sample, state_conv, state_rg_conv, state_rg_h,
              mix_norm, ffn_norm, ple_norm, final_norm,
              sc_w_in, sc_w_conv, sc_w_out,
              rg_w_x, rg_w_gate, rg_conv_w, rg_conv_b, rg_w_a, rg_b_a, rg_w_i, rg_b_i, rg_lambda, rg_w_out,
              ffn_w_gate, ffn_w_up, ffn_w_down, ple_w_gate, ple_w_proj):
    bsz = x_prompt.shape[0]
    dt = x_prompt.dtype
    conv0 = jnp.zeros((N_CONV_LAYERS, bsz, SC_WIDTH - 1, D_MODEL), dt)
    rgc0 = jnp.zeros((N_LRU_LAYERS, bsz, RG_CONV_WIDTH - 1, D_RNN), dt)
    rgh0 = jnp.zeros((N_LRU_LAYERS, bsz, D_RNN), dt)
    y_prompt, conv_p, rgc_p, rgh_p = trunk(
        x_prompt, p_prompt, conv0, rgc0, rgh0,
        mix_norm, ffn_norm, ple_norm, final_norm, sc_w_in, sc_w_conv, sc_w_out,
        rg_w_x, rg_w_gate, rg_conv_w, rg_conv_b, rg_w_a, rg_b_a, rg_w_i, rg_b_i, rg_lambda, rg_w_out,
        ffn_w_gate, ffn_w_up, ffn_w_down, ple_w_gate, ple_w_proj)
    y_sample, conv_s, rgc_s, rgh_s = trunk(
        x_sample, p_sample, state_conv, state_rg_conv, state_rg_h,
        mix_norm, ffn_norm, ple_norm, final_norm, sc_w_in, sc_w_conv, sc_w_out,
        rg_w_x, rg_w_gate, rg_conv_w, rg_conv_b, rg_w_a, rg_b_a, rg_w_i, rg_b_i, rg_lambda, rg_w_out,
        ffn_w_gate, ffn_w_up, ffn_w_down, ple_w_gate, ple_w_proj)
    return (y_prompt, y_sample, conv_p, conv_s, rgc_p, rgc_s, rgh_p, rgh_s)
```

```python
import numpy as np
from contextlib import ExitStack
import concourse.bass as bass
import concourse.mybir as mybir
from concourse.bass_utils import run_bass_kernel_spmd

F32 = mybir.dt.float32
BF16 = mybir.dt.bfloat16
AF = mybir.ActivationFunctionType
COPY = AF.Identity
ALU = mybir.AluOpType

D = 2048
NCH = 16
TP = 1024
NS = 16
T = TP + NS
TILES = [(0, 347), (347, 347), (694, 346)]
PTILES = [(0, 347), (347, 347), (694, 330)]
DFF = 5632
NQ = 4
QCH = 11
EPS = 1e-6
NV = 35
SLOT = 4096
NSLOT = 3
DECLARED = []
DBG = set()

V_MIX, V_FFN, V_PLE, V_FIN, V_SCW, V_RGW, V_RGB, V_BA, V_BI, V_LAM = 0, 4, 8, 12, 13, 19, 27, 29, 31, 33


class Eng:
    def __init__(self, name, sem):
        self.name = name
        self.sem = sem
        self.cnt = 0
        self.seen = {}
        self.prog = []


class SemSrc:
    def __init__(self, name, sem):
        self.name = name
        self.sem = sem
        self.cnt = 0


class Buf:
    __slots__ = ("w", "r", "const", "consumed")

    def __init__(self, const=False):
        self.w = None
        self.r = []
        self.const = const
        self.consumed = True


class Prog:
    def __init__(self, nc, st):
        self.nc = nc
        self.st = st
        self.E = {}
        for n in ("pe", "act", "dve", "pool", "sp"):
            self.E[n] = Eng(n, st.enter_context(nc.semaphore("sem_" + n)))
        self.nsem = 0

    def semsrc(self, name):
        self.nsem += 1
        return SemSrc(name, self.st.enter_context(self.nc.semaphore("ds_%s_%d" % (name, self.nsem))))

    def _deps(self, E, R, W):
        deps = {}

        def add(d):
            if d is None:
                return
            src, tk = d
            if deps.get(src, 0) < tk:
                deps[src] = tk
        for b in R:
            add(b.w)
        for b in W:
            add(b.w)
            for d in b.r:
                add(d)
        waits = []
        for src, tk in deps.items():
            if src is E and E.name == "pe":
                continue
            if E.seen.get(src, 0) >= tk:
                continue
            E.seen[src] = tk
            waits.append((src.sem, tk))
        return waits

    def op(self, en, fn, R=(), W=(), tag=None):
        if tag is not None and ("skip:" + tag) in DBG:
            return
        E = self.E[en]
        waits = self._deps(E, R, W)
        E.cnt += 1
        my = (E, E.cnt)
        sem = E.sem

        def run(h, waits=waits, fn=fn, sem=sem):
            for s_, v in waits:
                h.wait_ge(s_, v)
            fn(h).then_inc(sem, 1)
        E.prog.append(run)
        for b in R:
            b.consumed = True
            if not b.const:
                b.r.append(my)
        for b in W:
            b.w = my
            b.r = []
            b.consumed = False

    def dma(self, qn, S, out, in_, R=(), W=()):
        Q = self.E[qn]
        waits = self._deps(Q, R, W)
        if S.cnt > 0 and Q.seen.get(S, 0) < S.cnt:
            Q.seen[S] = S.cnt
            waits.append((S.sem, S.cnt))
        S.cnt += 16
        my = (S, S.cnt)
        sem = S.sem

        def run(h, waits=waits, sem=sem, out=out, in_=in_):
            for s_, v in waits:
                h.wait_ge(s_, v)
            h.dma_start(out=out, in_=in_).then_inc(sem, 16)
        Q.prog.append(run)
        for b in R:
            if not b.const:
                b.r.append(my)
        for b in W:
            b.w = my
            b.r = []

    def wait_all(self, en, srcs):
        E = self.E[en]
        waits = [(s.sem, s.cnt) for s in srcs if s.cnt > 0]

        def run(h, waits=waits):
            for s_, v in waits:
                h.wait_ge(s_, v)
        E.prog.append(run)


def _split_cols(n):
    for b in (2048, 1408, 1040, 1024, 512):
        if n % b == 0 and b <= n:
            return b
    raise ValueError(n)


def build_program(n_cores=8, stop=None):
    del DECLARED[:]
    nc = bass.Bass("TRN2", target_bir_lowering=False)

    def din(name, shape):
        return nc.dram_tensor(name, list(shape), F32, kind="ExternalInput").ap()

    def dout(name, shape):
        return nc.dram_tensor(name, list(shape), F32, kind="ExternalOutput").ap()

    xT = din("xT", [128, NCH * T])
    pT = din("pT", [4, 128, 2 * T])
    vecs = din("vecs", [128, NV * NCH])
    st_conv = din("st_conv", [2, 128, NCH * 2 * NS])
    st_rgc = din("st_rgc", [2, 128, NCH * 3 * NS])
    st_rgh = din("st_rgh", [2, 128, NCH * NS])
    flag_d = din("flag", [128, 1])
    WSHAPES = {
        "w_sccx": [2, 16, 128, 4096],
        "w_scb": [2, 8, 128, 4096],
        "w_scout": [2, 8, 128, 4096],
        "w_rgx": [2, 8, 128, 4096],
        "w_rgg": [2, 8, 128, 4096],
        "w_rgai": [2, 8, 128, 1024],
        "w_rgout": [2, 8, 128, 4096],
        "w_gu": [4, 44, 128, 4096],
        "w_dn": [4, NQ, 8, 128, QCH * 256],
        "w_pg": [4, 8, 128, 4096],
        "w_pp": [4, 8, 128, 512],
    }
    wcache = {}

    def wd(name):
        if name not in wcache:
            wcache[name] = din(name, WSHAPES[name])
            DECLARED.append(name)
        return wcache[name]

    y_d = dout("y", [128, NCH * T])
    o_conv = dout("o_conv", [2, 128, NCH * 2 * 17])
    o_rgc = dout("o_rgc", [2, 128, NCH * 3 * 17])
    o_rgh = dout("o_rgh", [2, 128, NCH * 17])

    ccsrc = [nc.dram_tensor("ccsrc%d" % l, [128, NCH * 4], F32, kind="Internal").ap() for l in range(4)]
    ccdst = [nc.dram_tensor("ccdst%d" % l, [256, NCH * 4], F32, kind="Internal", addr_space="Local").ap()
             for l in range(4)]

    with ExitStack() as st:
        def sb(name, shape, dt=F32):
            return st.enter_context(nc.sbuf_tensor(name, list(shape), dt))

        X = sb("X", [128, NCH, T])
        HN = sb("HN", [128, NCH, T], BF16)
        Z = sb("Z", [128, NCH, T], BF16)
        RING = [sb("RING%d" % i, [128, SLOT], BF16) for i in range(NSLOT)]
        VEC = sb("VEC", [128, NV, NCH])
        NLC = sb("NLC", [128, 2, NCH])
        HBV = sb("HBV", [128, 2, 2, NCH])
        STC = sb("STC", [128, NCH, 2, NS])
        STR = sb("STR", [128, NCH, 3, NS])
        STH = sb("STH", [128, NCH, NS])
        OSC = sb("OSC", [128, NCH, 2, 17])
        OSR = sb("OSR", [128, NCH, 3, 17])
        OSH = sb("OSH", [128, NCH, 17])
        ONES = sb("ONES", [128, 128], BF16)
        EXS = sb("EXS", [128, NCH, 4])
        EXR = sb("EXR", [128, NCH, 4])
        HALO = sb("HALO", [128, NCH, 4])
        YS = sb("YS", [128, NCH, 2])
        BS = sb("BS", [128, NCH, 2])
        FT = sb("FT", [128, 4, NCH])
        FLAG = sb("FLAG", [128, 1])
        UX = sb("UX", [128, 3 + T])
        U = [sb("U%d" % i, [128, T]) for i in range(2)]
        UB = sb("UB", [128, 2, T], BF16)
        Rb = sb("Rb", [128, T])
        Ib = sb("Ib", [128, T])
        Mb = sb("Mb", [128, T])
        Gt = sb("Gt", [128, 2, 512])
        PS = [st.enter_context(nc.psum_tensor("PS%d" % i, [128, 512], F32)) for i in range(8)]

        P = Prog(nc, st)
        op, dma = P.op, P.dma

        bX = [[Buf() for _ in TILES] for _ in range(NCH)]
        bHN = [[Buf() for _ in TILES] for _ in range(NCH)]
        bZ = [[Buf() for _ in TILES] for _ in range(NCH)]
        bRING = [Buf() for _ in range(NSLOT)]
        bVEC, bNLC, bONES, bFLAG = Buf(True), Buf(True), Buf(True), Buf(True)
        bSTC, bSTR, bSTH = Buf(), Buf(), Buf()
        bOSC, bOSR, bOSH = Buf(), Buf(), Buf()
        bEXS, bEXR, bHALO, bYS, bBS, bFT = Buf(), Buf(), Buf(), Buf(), Buf(), Buf()
        bUX = Buf()
        bU = [[Buf() for _ in TILES] for _ in range(2)]
        bUB = [[Buf() for _ in TILES] for _ in range(2)]
        bR = [Buf() for _ in TILES]
        bI = [Buf() for _ in TILES]
        bM = [Buf() for _ in TILES]
        bG = [Buf(), Buf()]
        bPS = [Buf() for _ in range(8)]
        bCS = [Buf() for _ in range(4)]
        bCD = [Buf() for _ in range(4)]
        bY = Buf()

        ring_sem = [P.semsrc("ring%d" % i) for i in range(NSLOT)]
        misc_sems = {"sp": [P.semsrc("msp%d" % i) for i in range(4)],
                     "pool": [P.semsrc("mpl%d" % i) for i in range(4)]}
        out_sems = []
        state = {"bank": 0, "slot": 0, "misc": 0, "g": 0}

        def bank():
            b = state["bank"]
            state["bank"] = (b + 1) % 8
            assert bPS[b].consumed, "PSUM bank %d re-allocated before its reader was emitted" % b
            return b

        def msem(q="sp"):
            s = misc_sems[q][state["misc"] % 4]
            state["misc"] += 1
            return s

        def load_slab(src2d, nelem, avoid=None):
            s = state["slot"]
            if s == avoid:
                s = (s + 1) % NSLOT
            state["slot"] = (s + 1) % NSLOT
            b = _split_cols(nelem)
            dma("pool", ring_sem[s],
                RING[s][:, 0:nelem].rearrange("p (a b) -> p a b", b=b),
                src2d.rearrange("p (a b) -> p a b", b=b), W=[bRING[s]])
            return s

        def mm_group(pbank, ncols, lhs_list, rhs_list, R, col0=0):
            n = len(lhs_list)

            def fn(h):
                ins = None
                for k in range(n):
                    ins = h.matmul(PS[pbank][:, col0:col0 + ncols], lhsT=lhs_list[k], rhs=rhs_list[k],
                                   start=(k == 0), stop=(k == n - 1))
                return ins
            op("pe", fn, R=R, W=[bPS[pbank]])

        def vcol(idx, c):
            return VEC[:, idx, c:c + 1]

        dma("sp", msem(), VEC[:].rearrange("p a b -> p (a b)"), vecs, W=[bVEC])
        dma("sp", msem(), FLAG[:], flag_d, W=[bFLAG])
        sx = P.semsrc("xload")
        for c in range(NCH):
            pass
        xT3 = xT.rearrange("p (a b) -> p a b", b=T)
        for ti, (t0, tn) in enumerate(TILES):
            sx = P.semsrc("xload%d" % ti)
            dma("sp", sx, X[:, :, t0:t0 + tn], xT3[:, :, t0:t0 + tn], W=[bX[c][ti] for c in range(NCH)])
        op("dve", lambda h: h.memset(ONES[:], 1.0), W=[bONES])
        op("dve", lambda h: h.memset(EXS[:].rearrange("p a b -> p (a b)"), 0.0), W=[bEXS])
        for j in range(2):
            op("act", lambda h, j=j: h.activation(out=NLC[:, j, :], in_=VEC[:, V_LAM + j, :], func=AF.Exp, scale=-1.0),
               R=[bVEC], W=[bNLC])
            op("act", lambda h, j=j: h.activation(out=NLC[:, j, :], in_=NLC[:, j, :], func=AF.Ln, bias=1.0),
               R=[bNLC], W=[bNLC])
            op("dve", lambda h, j=j: h.tensor_scalar(out=NLC[:, j, :], in0=NLC[:, j, :], scalar1=-4.0, scalar2=None,
                                                     op0=ALU.mult), R=[bNLC], W=[bNLC])

        for j in range(2):
            for q_, vi in ((0, V_BA), (1, V_BI)):
                op("dve", lambda h, j=j, q_=q_, vi=vi: h.tensor_scalar(out=HBV[:, j, q_, :], in0=VEC[:, vi + j, :], scalar1=0.5,
                                                                       scalar2=None, op0=ALU.mult), R=[bVEC], W=[bVEC])
        def norm(gidx, dst="HN"):
            for ti, (t0, tn) in enumerate(TILES):
                pb = bank()
                for c in range(NCH):
                    q = c % 4
                    h_, tq = q // 2, q % 2
                    sqap = UB[:, h_, TILES[tq][0]: TILES[tq][0] + tn]
                    op("act", lambda h, c=c, sqap=sqap, t0=t0, tn=tn: h.activation(
                        out=sqap, in_=X[:, c, t0:t0 + tn], func=AF.Square),
                        R=[bX[c][ti]], W=[bUB[h_][tq]])

                    def fn(h, c=c, sqap=sqap, pb=pb, tn=tn):
                        return h.matmul(PS[pb][:, 0:tn], lhsT=ONES[:], rhs=sqap, start=(c == 0), stop=(c == NCH - 1))
                    op("pe", fn, R=[bUB[h_][tq], bONES], W=[bPS[pb]] if c == 0 else [bPS[pb]])
                rs = Gt[:, 0, 0:tn]
                op("act", lambda h, pb=pb, tn=tn, rs=rs: h.activation(out=rs, in_=PS[pb][:, 0:tn], func=AF.Sqrt,
                                                                       scale=1.0 / D, bias=EPS),
                   R=[bPS[pb]], W=[bG[0]])
                op("dve", lambda h, rs=rs: h.reciprocal(out=rs, in_=rs), R=[bG[0]], W=[bG[0]])
                for c in range(NCH):
                    if dst == "HN":
                        o = HN[:, c, t0:t0 + tn]
                        Wb = [bHN[c][ti]]
                    else:
                        o = X[:, c, t0:t0 + tn]
                        Wb = [bX[c][ti]]
                    op("dve", lambda h, c=c, o=o, t0=t0, tn=tn, rs=rs: h.scalar_tensor_tensor(
                        out=o, in0=X[:, c, t0:t0 + tn], scalar=vcol(gidx, c), in1=rs, op0=ALU.mult, op1=ALU.mult),
                        R=[bX[c][ti], bG[0], bVEC], W=Wb)

        def out_proj(wd, nk, zb, pre=(), order=(0, 1, 2)):
            if "nooutproj" in DBG:
                return
            for n in range(8):
                s = pre[n] if n < len(pre) else load_slab(wd[n], nk * 256)
                for cc in range(2):
                    ch = 2 * n + cc
                    for ti in order:
                        t0, tn = TILES[ti]
                        pb = bank()
                        lhs = [RING[s][:, kc * 256 + cc * 128: kc * 256 + cc * 128 + 128] for kc in range(nk)]
                        rhs = [Z[:, kc, t0:t0 + tn] for kc in range(nk)]
                        mm_group(pb, tn, lhs, rhs, R=[bRING[s]] + [zb[kc][ti] for kc in range(nk)])
                        op("dve", lambda h, ch=ch, t0=t0, tn=tn, pb=pb: h.tensor_tensor(
                            out=X[:, ch, t0:t0 + tn], in0=PS[pb][:, 0:tn], in1=X[:, ch, t0:t0 + tn], op=ALU.add),
                            R=[bPS[pb]], W=[bX[ch][ti]])

        def exchange(l):
            if "noexch" in DBG:
                op("dve", lambda h: h.memset(HALO[:].rearrange("p a b -> p (a b)"), 0.0), W=[bHALO])
                return
            S1 = msem("pool")
            dma("pool", S1, ccsrc[l], EXS[:].rearrange("p a b -> p (a b)"), R=[bEXS], W=[bCS[l]])
            op("pool", lambda h, l=l: h.collective_compute(
                "AllGather", ALU.bypass, replica_groups=[[2 * i, 2 * i + 1] for i in range(n_cores // 2)],
                ins=[ccsrc[l]], outs=[ccdst[l]]), R=[bCS[l]], W=[bCD[l]])
            S2 = msem("pool")
            dma("pool", S2, EXR[:].rearrange("p a b -> p (a b)"), ccdst[l][0:128, :], R=[bCD[l]], W=[bEXR])
            op("dve", lambda h: h.tensor_scalar(out=HALO[:].rearrange("p a b -> p (a b)"),
                                                in0=EXR[:].rearrange("p a b -> p (a b)"),
                                                scalar1=FLAG[:, 0:1], scalar2=None, op0=ALU.mult),
               R=[bEXR, bFLAG], W=[bHALO])

        def conv_mixer(l):
            j = l // 2
            iw = V_SCW + 3 * j
            dma("sp", msem(), STC[:].rearrange("p a b c -> p (a b c)"), st_conv[j], W=[bSTC])
            CX = UX
            op("dve", lambda h: h.memset(CX[:, 0:2], 0.0), W=[bUX])
            sb_ = None
            for f in range(NCH if "nf" not in DBG else 2):
                s_cx = load_slab(wd("w_sccx")[j, f], 4096)
                if f % 2 == 0:
                    sb_ = load_slab(wd("w_scb")[j, f // 2], 4096)
                cc = f % 2
                pcs, pxs, pbks = {}, {}, {}

                def pe_b(ti):
                    t0p, tnp = TILES[ti]
                    pbks[ti] = bank()
                    mm_group(pbks[ti], tnp, [RING[sb_][:, kc * 256 + cc * 128: kc * 256 + cc * 128 + 128]
                                             for kc in range(NCH)], [HN[:, kc, t0p:t0p + tnp] for kc in range(NCH)],
                             R=[bRING[sb_]] + [bHN[kc][ti] for kc in range(NCH)])

                def ew_A(ti):
                    t0, tn = TILES[ti]
                    pc, px = pcs[ti], pxs[ti]
                    csb = U[0][:, t0:t0 + tn]
                    tmp = U[1][:, t0:t0 + tn]
                    op("act", lambda h, csb=csb, pc=pc, tn=tn: h.activation(out=csb, in_=PS[pc][:, 0:tn], func=COPY),
                       R=[bPS[pc]], W=[bU[0][ti]], tag="evac")
                    cxo = CX[:, 2 + t0: 2 + t0 + tn]
                    op("dve", lambda h, cxo=cxo, csb=csb, px=px, tn=tn: h.tensor_tensor(
                        out=cxo, in0=PS[px][:, 0:tn], in1=csb, op=ALU.mult),
                        R=[bPS[px], bU[0][ti]], W=[bUX], tag="cx")
                    pn = min(tn, TP - t0)
                    has_s = t0 + tn > TP
                    ptmp = U[1][:, t0:t0 + pn]
                    op("dve", lambda h, ptmp=ptmp, t0=t0, pn=pn, f=f: h.tensor_scalar(
                        out=ptmp, in0=CX[:, t0:t0 + pn], scalar1=vcol(iw, f), scalar2=None, op0=ALU.mult),
                        R=[bUX, bVEC], W=[bU[1][ti]], tag="conv")
                    for k in (1, 2):
                        op("dve", lambda h, ptmp=ptmp, t0=t0, pn=pn, f=f, k=k: h.scalar_tensor_tensor(
                            out=ptmp, in0=CX[:, t0 + k:t0 + k + pn], scalar=vcol(iw + k, f), in1=ptmp,
                            op0=ALU.mult, op1=ALU.add), R=[bUX, bU[1][ti], bVEC], W=[bU[1][ti]], tag="conv")
                    if has_s:
                        stmp = U[1][:, TP:T]
                        cxs = CX[:, 2 + TP:2 + T]
                        op("dve", lambda h, stmp=stmp, f=f: h.tensor_scalar(
                            out=stmp, in0=STC[:, f, 0, :], scalar1=vcol(iw, f), scalar2=None, op0=ALU.mult),
                            R=[bSTC, bVEC], W=[bU[1][ti]], tag="conv")
                        op("dve", lambda h, stmp=stmp, f=f: h.scalar_tensor_tensor(
                            out=stmp, in0=STC[:, f, 1, :], scalar=vcol(iw + 1, f), in1=stmp, op0=ALU.mult, op1=ALU.add),
                            R=[bSTC, bU[1][ti], bVEC], W=[bU[1][ti]], tag="conv")
                        op("dve", lambda h, stmp=stmp, f=f, cxs=cxs: h.scalar_tensor_tensor(
                            out=stmp, in0=cxs, scalar=vcol(iw + 2, f), in1=stmp, op0=ALU.mult, op1=ALU.add),
                            R=[bUX, bU[1][ti], bVEC], W=[bU[1][ti]], tag="conv")
                        op("act", lambda h, f=f: h.activation(out=OSC[:, f, 0, 1:17], in_=STC[:, f, 1, :], func=COPY),
                           R=[bSTC], W=[bOSC], tag="small1")
                        op("act", lambda h, f=f, cxs=cxs: h.activation(out=OSC[:, f, 1, 1:17], in_=cxs, func=COPY),
                           R=[bUX], W=[bOSC], tag="small1")
                    if has_s:
                        op("dve", lambda h, f=f: h.tensor_copy(out=EXS[:, f, 0:2], in_=CX[:, TP:TP + 2]),
                           R=[bUX], W=[bEXS], tag="small2")
                        op("dve", lambda h, f=f: h.tensor_copy(out=OSC[:, f, :, 0], in_=CX[:, TP:TP + 2]),
                           R=[bUX], W=[bOSC], tag="small3")

                def ew_B(ti):
                    t0, tn = TILES[ti]
                    pbk = pbks[ti]
                    tmp = U[1][:, t0:t0 + tn]
                    if ti == 0:
                        op("dve", lambda h, f=f: h.tensor_copy(out=YS[:, f, :], in_=U[1][:, 0:2]),
                           R=[bU[1][ti]], W=[bYS], tag="small2")
                        op("dve", lambda h, f=f, pbk=pbk: h.tensor_copy(out=BS[:, f, :], in_=PS[pbk][:, 0:2]),
                           R=[bPS[pbk]], W=[bBS], tag="small2")
                    op("dve", lambda h, f=f, t0=t0, tn=tn, tmp=tmp, pbk=pbk: h.tensor_tensor(
                        out=Z[:, f, t0:t0 + tn], in0=PS[pbk][:, 0:tn], in1=tmp, op=ALU.mult),
                        R=[bPS[pbk], bU[1][ti]], W=[bZ[f][ti]], tag="z")


                for ti, (t0, tn) in enumerate(TILES):
                    pcs[ti], pxs[ti] = bank(), bank()
                    hn = [HN[:, kc, t0:t0 + tn] for kc in range(NCH)]
                    Rh = [bHN[kc][ti] for kc in range(NCH)]
                    mm_group(pcs[ti], tn, [RING[s_cx][:, kc * 256: kc * 256 + 128] for kc in range(NCH)], hn,
                             R=[bRING[s_cx]] + Rh)
                    mm_group(pxs[ti], tn, [RING[s_cx][:, kc * 256 + 128: kc * 256 + 256] for kc in range(NCH)], hn,
                             R=[bRING[s_cx]] + Rh)
                    ew_A(ti)
                    if ti > 0:
                        pe_b(ti - 1)
                        ew_B(ti - 1)
                pe_b(2)
                ew_B(2)
            so = P.semsrc("oc%d" % j)
            out_sems.append(so)
            dma("sp", so, o_conv[j], OSC[:].rearrange("p a b c -> p (a b c)"), R=[bOSC])
            pre = [load_slab(wd("w_scout")[j][n_], NCH * 256) for n_ in range(2)]
            exchange(l)
            w0 = VEC[:, iw, :]
            w1 = VEC[:, iw + 1, :]
            h0 = HALO[:, :, 0]
            h1 = HALO[:, :, 1]
            op("dve", lambda h: h.tensor_tensor(out=FT[:, 0, :], in0=w0, in1=h0, op=ALU.mult), R=[bHALO, bVEC], W=[bFT])
            op("dve", lambda h: h.tensor_tensor(out=FT[:, 1, :], in0=w1, in1=h1, op=ALU.mult), R=[bHALO, bVEC, bFT], W=[bFT])
            op("dve", lambda h: h.tensor_tensor(out=FT[:, 0, :], in0=FT[:, 0, :], in1=FT[:, 1, :], op=ALU.add), R=[bFT], W=[bFT])
            op("dve", lambda h: h.tensor_tensor(out=FT[:, 2, :], in0=w0, in1=h1, op=ALU.mult), R=[bHALO, bVEC, bFT], W=[bFT])
            op("dve", lambda h: h.tensor_tensor(out=YS[:, :, 0], in0=YS[:, :, 0], in1=FT[:, 0, :], op=ALU.add), R=[bFT, bYS], W=[bYS])
            op("dve", lambda h: h.tensor_tensor(out=YS[:, :, 1], in0=YS[:, :, 1], in1=FT[:, 2, :], op=ALU.add), R=[bFT, bYS], W=[bYS])
            op("dve", lambda h: h.tensor_tensor(out=Z[:, :, 0:2], in0=YS[:], in1=BS[:], op=ALU.mult),
               R=[bYS, bBS], W=[bZ[f][0] for f in range(NCH)])
            out_proj(wd("w_scout")[j], NCH, bZ, pre=pre, order=(1, 2, 0))

        def rg_pass(l, final, pre=None):
            j = l // 2
            iw = V_RGW + 4 * j
            ib = V_RGB + j
            tiles = list(enumerate(TILES)) if final else list(enumerate(TILES))[:2]
            for n in range(8):
                if n == 0 and pre is not None:
                    s_x, s_ai, s_g = pre
                else:
                    s_x = load_slab(wd("w_rgx")[j, n], 4096)
                    s_ai = load_slab(wd("w_rgai")[j, n], 1024)
                    s_g = load_slab(wd("w_rgg")[j, n], 4096) if final else None
                for cc in range(2):
                    f = 2 * n + cc
                    if final:
                        op("dve", lambda h, f=f: h.tensor_copy(out=UX[:, 0:3], in_=HALO[:, f, 0:3]),
                           R=[bHALO], W=[bUX])
                    else:
                        op("dve", lambda h: h.memset(UX[:, 0:3], 0.0), W=[bUX])
                    pbs, pgs = {}, {}
                    hnr = lambda t0, tn: [HN[:, kc, t0:t0 + tn] for kc in range(NCH)]
                    for ti, (t0, tn) in tiles:
                        pbs[ti] = bank()
                        mm_group(pbs[ti], tn, [RING[s_x][:, kc * 256 + cc * 128: kc * 256 + cc * 128 + 128] for kc in range(NCH)],
                                 hnr(t0, tn), R=[bRING[s_x]] + [bHN[kc][ti] for kc in range(NCH)])
                    for ti, (t0, tn) in tiles:
                        op("act", lambda h, t0=t0, tn=tn, pb=pbs[ti]: h.activation(out=UX[:, 3 + t0:3 + t0 + tn], in_=PS[pb][:, 0:tn], func=COPY),
                           R=[bPS[pbs[ti]]], W=[bUX])
                    bUall = [bU[cc][0], bU[cc][1], bU[cc][2]]
                    op("act", lambda h, cc=cc, f=f: h.activation(out=U[cc][:, 0:TP], in_=UX[:, 0:TP], func=COPY,
                                                                  scale=vcol(iw, f), bias=vcol(ib, f)),
                       R=[bUX, bVEC], W=bUall)
                    for k in (1, 2, 3):
                        op("dve", lambda h, cc=cc, f=f, k=k: h.scalar_tensor_tensor(
                            out=U[cc][:, 0:TP], in0=UX[:, k:k + TP], scalar=vcol(iw + k, f), in1=U[cc][:, 0:TP],
                            op0=ALU.mult, op1=ALU.add), R=[bUX, bVEC] + bUall, W=bUall)
                    for ti, (t0, tn) in tiles:
                        if t0 + tn > TP:
                            uo = U[cc][:, TP:T]
                            uxo = UX[:, 3 + TP:3 + T]
                            op("dve", lambda h, uo=uo, f=f: h.tensor_scalar(
                                out=uo, in0=STR[:, f, 0, :], scalar1=vcol(iw, f), scalar2=vcol(ib, f),
                                op0=ALU.mult, op1=ALU.add), R=[bSTR, bVEC], W=[bU[cc][ti]])
                            for k in (1, 2):
                                op("dve", lambda h, uo=uo, f=f, k=k: h.scalar_tensor_tensor(
                                    out=uo, in0=STR[:, f, k, :], scalar=vcol(iw + k, f), in1=uo,
                                    op0=ALU.mult, op1=ALU.add), R=[bSTR, bU[cc][ti], bVEC], W=[bU[cc][ti]])
                            op("dve", lambda h, uo=uo, f=f, uxo=uxo: h.scalar_tensor_tensor(
                                out=uo, in0=uxo, scalar=vcol(iw + 3, f), in1=uo, op0=ALU.mult, op1=ALU.add),
                                R=[bUX, bU[cc][ti], bVEC], W=[bU[cc][ti]])
                            op("act", lambda h, f=f: h.activation(out=OSR[:, f, 0:2, 1:17], in_=STR[:, f, 1:3, :], func=COPY),
                               R=[bSTR], W=[bOSR])
                            op("act", lambda h, f=f, uxo=uxo: h.activation(out=OSR[:, f, 2, 1:17], in_=uxo, func=COPY),
                               R=[bUX], W=[bOSR])
                    op("act", lambda h, cc=cc: h.activation(out=UB[:, cc, 0:T], in_=U[cc][:, 0:T], func=COPY),
                       R=bUall, W=[bUB[cc][0], bUB[cc][1], bUB[cc][2]])
                    if final:
                        op("dve", lambda h, f=f: h.tensor_copy(out=OSR[:, f, :, 0], in_=UX[:, TP:TP + 3]), R=[bUX], W=[bOSR])
                    else:
                        op("dve", lambda h, f=f: h.tensor_copy(out=EXS[:, f, 0:3], in_=UX[:, TP:TP + 3]), R=[bUX], W=[bEXS])
                if final:
                    for cc in range(2):
                        f = 2 * n + cc
                        pgs = {}
                        for ti, (t0, tn) in tiles:
                            pgs[ti] = bank()
                            mm_group(pgs[ti], tn, [RING[s_g][:, kc * 256 + cc * 128: kc * 256 + cc * 128 + 128] for kc in range(NCH)],
                                     [HN[:, kc, t0:t0 + tn] for kc in range(NCH)], R=[bRING[s_g]] + [bHN[kc][ti] for kc in range(NCH)])
                        for ti, (t0, tn) in tiles:
                            op("act", lambda h, f=f, t0=t0, tn=tn, pg=pgs[ti]: h.activation(out=Z[:, f, t0:t0 + tn], in_=PS[pg][:, 0:tn], func=AF.Gelu_apprx_tanh),
                               R=[bPS[pgs[ti]], bHALO], W=[bZ[f][ti]])
                for oc in range(2):
                    f = 2 * n + oc
                    prs, pis = {}, {}
                    for ti, (t0, tn) in tiles:
                        prs[ti], pis[ti] = bank(), bank()
                        rhs = [UB[:, kc, t0:t0 + tn] for kc in range(2)]
                        Ru = [bRING[s_ai], bUB[0][ti], bUB[1][ti]]
                        mm_group(prs[ti], tn, [RING[s_ai][:, kc * 256 + oc * 128: kc * 256 + oc * 128 + 128] for kc in range(2)], rhs, R=Ru)
                        mm_group(pis[ti], tn, [RING[s_ai][:, 512 + kc * 256 + oc * 128: 512 + kc * 256 + oc * 128 + 128] for kc in range(2)], rhs, R=Ru)
                    sl = lambda B_, t0, tn: B_[:, t0:t0 + tn]
                    for ti, (t0, tn) in tiles:
                        op("act", lambda h, t0=t0, tn=tn, pr=prs[ti], f=f: h.activation(out=sl(Rb, t0, tn), in_=PS[pr][:, 0:tn], func=AF.Tanh,
                                                                                      scale=0.5, bias=HBV[:, j, 0, f:f + 1]),
                           R=[bPS[prs[ti]], bVEC], W=[bR[ti]])
                    for ti, (t0, tn) in tiles:
                        op("act", lambda h, t0=t0, tn=tn, pi=pis[ti], f=f: h.activation(out=sl(Ib, t0, tn), in_=PS[pi][:, 0:tn], func=AF.Tanh,
                                                                                      scale=0.5, bias=HBV[:, j, 1, f:f + 1]),
                           R=[bPS[pis[ti]], bVEC], W=[bI[ti]])
                    bRa, bIa, bMa = list(bR), list(bI), list(bM)
                    op("act", lambda h, f=f: h.activation(out=Rb[:, 0:T], in_=Rb[:, 0:T], func=AF.Exp,
                                                          scale=NLC[:, j, f:f + 1], bias=NLC[:, j, f:f + 1]),
                       R=bRa + [bNLC], W=bRa)
                    op("act", lambda h: h.activation(out=Mb[:, 0:T], in_=Rb[:, 0:T], func=AF.Square), R=bRa, W=bMa)
                    op("act", lambda h: h.activation(out=Mb[:, 0:T], in_=Mb[:, 0:T], func=AF.Sqrt, scale=-0.25, bias=0.25), R=bMa, W=bMa)
                    op("dve", lambda h: h.scalar_tensor_tensor(out=Ib[:, 0:T], in0=Ib[:, 0:T], scalar=1.0, in1=Mb[:, 0:T],
                                                               op0=ALU.add, op1=ALU.mult), R=bIa + bMa, W=bIa)
                    op("dve", lambda h, oc=oc: h.tensor_tensor(out=Ib[:, 0:T], in0=Ib[:, 0:T], in1=U[oc][:, 0:T], op=ALU.mult),
                       R=bIa + [bU[oc][0], bU[oc][1], bU[oc][2]], W=bIa)
                    init = HALO[:, f, 3:4] if final else 0.0
                    op("dve", lambda h, init=init: h.tensor_tensor_scan(out=Mb[:, 0:TP], data0=Rb[:, 0:TP], data1=Ib[:, 0:TP],
                                                                         initial=init, op0=ALU.mult, op1=ALU.add),
                       R=[bR[0], bR[1], bR[2], bI[0], bI[1], bI[2], bHALO], W=[bM[0], bM[1], bM[2]])
                    if not final:
                        op("dve", lambda h, f=f: h.tensor_copy(out=EXS[:, f, 3:4], in_=Mb[:, TP - 1:TP]),
                           R=[bM[2]], W=[bEXS])
                        continue
                    op("dve", lambda h, f=f: h.tensor_copy(out=OSH[:, f, 0:1], in_=Mb[:, TP - 1:TP]), R=[bM[2]], W=[bOSH])
                    op("dve", lambda h, f=f: h.tensor_tensor(out=Mb[:, TP:T], in0=Rb[:, TP:T], in1=STH[:, f, :], op=ALU.mult),
                       R=[bR[2], bSTH], W=[bM[2]])
                    op("dve", lambda h: h.tensor_tensor(out=Mb[:, TP:T], in0=Mb[:, TP:T], in1=Ib[:, TP:T], op=ALU.add),
                       R=[bM[2], bI[2]], W=[bM[2]])
                    op("act", lambda h, f=f: h.activation(out=OSH[:, f, 1:17], in_=Mb[:, TP:T], func=COPY), R=[bM[2]], W=[bOSH])
                    op("dve", lambda h, f=f: h.tensor_tensor(out=Z[:, f, 0:T], in0=Z[:, f, 0:T], in1=Mb[:, 0:T], op=ALU.mult),
                       R=bMa + bZ[f], W=bZ[f])

        Zf = Z[:].bitcast(F32)

        def zview(c0, c1):
            return Zf[:, c0:c1, :].rearrange("p a b -> p (a b)")
        SETS = [
            dict(UX=UX[:], U=[U[0][:], U[1][:]], UB=UB[:], R=Rb[:], I=Ib[:], M=Mb[:],
                 bUX=bUX, bU=bU, bUB=bUB, bR=bR, bI=bI, bM=bM),
            dict(UX=zview(0, 3), U=[zview(3, 5), zview(5, 7)], UB=Z[:, 13:15, :], R=zview(7, 9), I=zview(9, 11), M=zview(11, 13),
                 bUX=Buf(), bU=[[Buf() for _ in TILES] for _ in range(2)], bUB=[[Buf() for _ in TILES] for _ in range(2)],
                 bR=[Buf() for _ in TILES], bI=[Buf() for _ in TILES], bM=[Buf() for _ in TILES]),
        ]

        def rg_block1(l, n, S, s_ai, aoff):
            j = l // 2
            iw = V_RGW + 4 * j
            ib = V_RGB + j
            tiles = list(enumerate(PTILES))
            UXs, Us, UBs, Rs, Is, Ms = S["UX"], S["U"], S["UB"], S["R"], S["I"], S["M"]
            s_x = load_slab(wd("w_rgx")[j, n], 4096, avoid=s_ai)
            for cc in range(2):
                f = 2 * n + cc
                op("dve", lambda h: h.memset(UXs[:, 0:3], 0.0), W=[S["bUX"]])
                pbs = {}
                for ti, (t0, tn) in tiles:
                    pbs[ti] = bank()
                    mm_group(pbs[ti], tn, [RING[s_x][:, kc * 256 + cc * 128: kc * 256 + cc * 128 + 128] for kc in range(NCH)],
                             [HN[:, kc, t0:t0 + tn] for kc in range(NCH)], R=[bRING[s_x]] + [bHN[kc][ti] for kc in range(NCH)])
                yield
                for ti, (t0, tn) in tiles:
                    op("act", lambda h, t0=t0, tn=tn, pb=pbs[ti]: h.activation(out=UXs[:, 3 + t0:3 + t0 + tn], in_=PS[pb][:, 0:tn], func=COPY),
                       R=[bPS[pbs[ti]]], W=[S["bUX"]])
                yield
                bUall = list(S["bU"][cc])
                op("act", lambda h, cc=cc, f=f: h.activation(out=Us[cc][:, 0:TP], in_=UXs[:, 0:TP], func=COPY,
                                                              scale=vcol(iw, f), bias=vcol(ib, f)),
                   R=[S["bUX"], bVEC], W=bUall)
                yield
                for k in (1, 2, 3):
                    op("dve", lambda h, cc=cc, f=f, k=k: h.scalar_tensor_tensor(
                        out=Us[cc][:, 0:TP], in0=UXs[:, k:k + TP], scalar=vcol(iw + k, f), in1=Us[cc][:, 0:TP],
                        op0=ALU.mult, op1=ALU.add), R=[S["bUX"], bVEC] + bUall, W=bUall)
                yield
                op("act", lambda h, cc=cc: h.activation(out=UBs[:, cc, 0:TP], in_=Us[cc][:, 0:TP], func=COPY),
                   R=bUall, W=list(S["bUB"][cc]))
                op("dve", lambda h, f=f: h.tensor_copy(out=EXS[:, f, 0:3], in_=UXs[:, TP:TP + 3]), R=[S["bUX"]], W=[bEXS])
                yield
            for oc in range(2):
                f = 2 * n + oc
                prs, pis = {}, {}
                for ti, (t0, tn) in tiles:
                    prs[ti], pis[ti] = bank(), bank()
                    rhs = [UBs[:, kc, t0:t0 + tn] for kc in range(2)]
                    Ru = [bRING[s_ai], S["bUB"][0][ti], S["bUB"][1][ti]]
                    mm_group(prs[ti], tn, [RING[s_ai][:, aoff + kc * 256 + oc * 128: aoff + kc * 256 + oc * 128 + 128] for kc in range(2)], rhs, R=Ru)
                    mm_group(pis[ti], tn, [RING[s_ai][:, aoff + 512 + kc * 256 + oc * 128: aoff + 512 + kc * 256 + oc * 128 + 128] for kc in range(2)], rhs, R=Ru)
                    op("act", lambda h, t0=t0, tn=tn, pr=prs[ti], f=f: h.activation(out=Rs[:, t0:t0 + tn], in_=PS[pr][:, 0:tn], func=AF.Tanh,
                                                                                  scale=0.5, bias=HBV[:, j, 0, f:f + 1]),
                       R=[bPS[prs[ti]], bVEC], W=[S["bR"][ti]])
                    op("act", lambda h, t0=t0, tn=tn, pi=pis[ti], f=f: h.activation(out=Is[:, t0:t0 + tn], in_=PS[pi][:, 0:tn], func=AF.Tanh,
                                                                                  scale=0.5, bias=HBV[:, j, 1, f:f + 1]),
                       R=[bPS[pis[ti]], bVEC], W=[S["bI"][ti]])
                yield
                bRa, bIa, bMa = list(S["bR"]), list(S["bI"]), list(S["bM"])
                op("act", lambda h, f=f: h.activation(out=Rs[:, 0:TP], in_=Rs[:, 0:TP], func=AF.Exp,
                                                      scale=NLC[:, j, f:f + 1], bias=NLC[:, j, f:f + 1]),
                   R=bRa + [bNLC], W=bRa)
                op("act", lambda h: h.activation(out=Ms[:, 0:TP], in_=Rs[:, 0:TP], func=AF.Square), R=bRa, W=bMa)
                yield
                op("act", lambda h: h.activation(out=Ms[:, 0:TP], in_=Ms[:, 0:TP], func=AF.Sqrt, scale=-0.25, bias=0.25), R=bMa, W=bMa)
                yield
                op("dve", lambda h: h.scalar_tensor_tensor(out=Is[:, 0:TP], in0=Is[:, 0:TP], scalar=1.0, in1=Ms[:, 0:TP],
                                                           op0=ALU.add, op1=ALU.mult), R=bIa + bMa, W=bIa)
                op("dve", lambda h, oc=oc: h.tensor_tensor(out=Is[:, 0:TP], in0=Is[:, 0:TP], in1=Us[oc][:, 0:TP], op=ALU.mult),
                   R=bIa + list(S["bU"][oc]), W=bIa)
                yield
                op("dve", lambda h: h.tensor_tensor_scan(out=Ms[:, 0:TP], data0=Rs[:, 0:TP], data1=Is[:, 0:TP],
                                                         initial=0.0, op0=ALU.mult, op1=ALU.add),
                   R=S["bR"] + S["bI"], W=S["bM"])
                op("dve", lambda h, f=f: h.tensor_copy(out=EXS[:, f, 3:4], in_=Ms[:, TP - 1:TP]), R=[S["bM"][2]], W=[bEXS])
                yield

        def rg_pass1(l):
            j = l // 2
            for q in range(2):
                s_ai = state["slot"]
                state["slot"] = (s_ai + 1) % NSLOT
                dma("pool", ring_sem[s_ai], RING[s_ai][:, 0:4096].rearrange("p (b e) -> p b e", e=1024),
                    wd("w_rgai")[j, 4 * q:4 * q + 4].rearrange("b p e -> p b e"), W=[bRING[s_ai]])
                for pair in range(2):
                    n0 = 4 * q + 2 * pair
                    gens = [rg_block1(l, n0, SETS[0], s_ai, (n0 % 4) * 1024),
                            rg_block1(l, n0 + 1, SETS[1], s_ai, ((n0 + 1) % 4) * 1024)]
                    alive = [True, True]
                    while any(alive):
                        for gi in range(2):
                            if alive[gi]:
                                try:
                                    next(gens[gi])
                                except StopIteration:
                                    alive[gi] = False

        def rg_mixer(l):
            j = l // 2
            dma("sp", msem(), STR[:].rearrange("p a b c -> p (a b c)"), st_rgc[j], W=[bSTR])
            dma("sp", msem(), STH[:].rearrange("p a b -> p (a b)"), st_rgh[j], W=[bSTH])
            rg_pass1(l)
            pre = (load_slab(wd("w_rgx")[j, 0], 4096), load_slab(wd("w_rgai")[j, 0], 1024), load_slab(wd("w_rgg")[j, 0], 4096))
            exchange(l)
            rg_pass(l, True, pre=pre)
            so1, so2 = P.semsrc("or%d" % j), P.semsrc("oh%d" % j)
            out_sems.extend([so1, so2])
            dma("sp", so1, o_rgc[j], OSR[:].rearrange("p a b c -> p (a b c)"), R=[bOSR])
            dma("sp", so2, o_rgh[j], OSH[:].rearrange("p a b -> p (a b)"), R=[bOSH])
            out_proj(wd("w_rgout")[j], NCH, bZ)

        def ffn(l):
            for q in range(NQ):
                for jj in range(QCH):
                    jx = q * QCH + jj
                    s = load_slab(wd("w_gu")[l, jx], 4096)
                    for ti, (t0, tn) in enumerate(TILES):
                        pg, pu = bank(), bank()
                        hn = [HN[:, kc, t0:t0 + tn] for kc in range(NCH)]
                        Rh = [bRING[s]] + [bHN[kc][ti] for kc in range(NCH)]
                        mm_group(pg, tn, [RING[s][:, kc * 256: kc * 256 + 128] for kc in range(NCH)], hn, R=Rh)
                        mm_group(pu, tn, [RING[s][:, kc * 256 + 128: kc * 256 + 256] for kc in range(NCH)], hn, R=Rh)
                        gi = state["g"] % 2
                        state["g"] += 1
                        g_ = Gt[:, gi, 0:tn]
                        op("act", lambda h, g_=g_, pg=pg, tn=tn: h.activation(out=g_, in_=PS[pg][:, 0:tn], func=AF.Silu),
                           R=[bPS[pg]], W=[bG[gi]])
                        op("dve", lambda h, g_=g_, jj=jj, t0=t0, tn=tn, pu=pu: h.tensor_tensor(
                            out=Z[:, jj, t0:t0 + tn], in0=PS[pu][:, 0:tn], in1=g_, op=ALU.mult),
                            R=[bPS[pu], bG[gi]], W=[bZ[jj][ti]])
                out_proj(wd("w_dn")[l, q], QCH, bZ)

        def ple(l):
            sp_ = msem("pool")
            dma("pool", sp_, UB[:].rearrange("p a b -> p (a b)").rearrange("p (a b) -> p a b", b=T),
                pT[l].rearrange("p (a b) -> p a b", b=T), W=[b for row in bUB for b in row])
            for n in range(8):
                s = load_slab(wd("w_pg")[l, n], 4096)
                s2 = load_slab(wd("w_pp")[l, n], 512)
                for cc in range(2):
                    ch = 2 * n + cc
                    for ti, (t0, tn) in enumerate(TILES):
                        pg, pp_ = bank(), bank()
                        mm_group(pg, tn, [RING[s][:, kc * 256 + cc * 128: kc * 256 + cc * 128 + 128] for kc in range(NCH)],
                                 [HN[:, kc, t0:t0 + tn] for kc in range(NCH)],
                                 R=[bRING[s]] + [bHN[kc][ti] for kc in range(NCH)])
                        mm_group(pp_, tn, [RING[s2][:, kc * 256 + cc * 128: kc * 256 + cc * 128 + 128] for kc in range(2)],
                                 [UB[:, kc, t0:t0 + tn] for kc in range(2)],
                                 R=[bRING[s2], bUB[0][ti], bUB[1][ti]])
                        gi = state["g"] % 2
                        state["g"] += 1
                        g_ = Gt[:, gi, 0:tn]
                        op("act", lambda h, g_=g_, pg=pg, tn=tn: h.activation(out=g_, in_=PS[pg][:, 0:tn], func=AF.Sigmoid),
                           R=[bPS[pg]], W=[bG[gi]])
                        op("dve", lambda h, g_=g_, pp_=pp_, tn=tn: h.tensor_tensor(out=g_, in0=PS[pp_][:, 0:tn], in1=g_, op=ALU.mult),
                           R=[bPS[pp_], bG[gi]], W=[bG[gi]])
                        op("dve", lambda h, g_=g_, ch=ch, t0=t0, tn=tn: h.tensor_tensor(
                            out=X[:, ch, t0:t0 + tn], in0=X[:, ch, t0:t0 + tn], in1=g_, op=ALU.add),
                            R=[bG[gi]], W=[bX[ch][ti]])

        phases = []
        for l in range(4):
            phases.append(lambda l=l: norm(V_MIX + l))
            phases.append((lambda l=l: conv_mixer(l)) if l % 2 == 0 else (lambda l=l: rg_mixer(l)))
            phases.append(lambda l=l: norm(V_FFN + l))
            phases.append(lambda l=l: ffn(l))
            phases.append(lambda l=l: norm(V_PLE + l))
            phases.append(lambda l=l: ple(l))
        phases.append(lambda: norm(V_FIN, dst="X"))
        for ph in (phases if stop is None else phases[:stop]):
            ph()
        y3 = y_d.rearrange("p (a b) -> p a b", b=T)
        for ti, (t0, tn) in enumerate(TILES):
            sy = P.semsrc("yout%d" % ti)
            out_sems.append(sy)
            dma("sp", sy, y3[:, :, t0:t0 + tn], X[:, :, t0:t0 + tn], R=[bX[c][ti] for c in range(NCH)])
        P.wait_all("sp", out_sems)

        with nc.Block() as block:
            @block.tensor
            def _(h):
                for f_ in P.E["pe"].prog:
                    f_(h)

            @block.scalar
            def _(h):
                for f_ in P.E["act"].prog:
                    f_(h)

            @block.vector
            def _(h):
                for f_ in P.E["dve"].prog:
                    f_(h)

            @block.gpsimd
            def _(h):
                for f_ in P.E["pool"].prog:
                    f_(h)

            @block.sync
            def _(h):
                for f_ in P.E["sp"].prog:
                    f_(h)
    return nc


def _slab(W, kc):
    n = W.shape[1]
    return W.reshape(kc, 128, n).transpose(1, 0, 2).reshape(128, kc * n)


def _feat_major(a):
    t, d = a.shape
    return a.T.reshape(d // 128, 128, t).transpose(1, 0, 2)


def _cols256(W):
    K = W.shape[0]
    kc = K // 128
    return np.ascontiguousarray(W.reshape(kc, 128, 8, 256).transpose(2, 1, 0, 3)).reshape(8, 128, kc * 256)


def prepare(x_prompt, x_sample, p_prompt, p_sample, state_conv, state_rg_conv, state_rg_h,
            mix_norm, ffn_norm, ple_norm, final_norm,
            sc_w_in, sc_w_conv, sc_w_out,
            rg_w_x, rg_w_gate, rg_conv_w, rg_conv_b, rg_w_a, rg_b_a, rg_w_i, rg_b_i, rg_lambda, rg_w_out,
            ffn_w_gate, ffn_w_up, ffn_w_down, ple_w_gate, ple_w_proj):
    f32 = np.float32
    A = lambda a: np.asarray(a, dtype=f32)
    x_prompt, x_sample, p_prompt, p_sample = A(x_prompt), A(x_sample), A(p_prompt), A(p_sample)
    state_conv, state_rg_conv, state_rg_h = A(state_conv), A(state_rg_conv), A(state_rg_h)

    vl = [A(mix_norm)[l] for l in range(4)] + [A(ffn_norm)[l] for l in range(4)] + [A(ple_norm)[l] for l in range(4)]
    vl += [A(final_norm)]
    vl += [A(sc_w_conv)[j, k] for j in range(2) for k in range(3)]
    vl += [A(rg_conv_w)[j, k] for j in range(2) for k in range(4)]
    vl += [A(rg_conv_b)[j] for j in range(2)] + [A(rg_b_a)[j] for j in range(2)] + [A(rg_b_i)[j] for j in range(2)]
    vl += [A(rg_lambda)[j] for j in range(2)]
    assert len(vl) == NV
    vecs = np.ascontiguousarray(np.stack(vl, 0).reshape(NV, NCH, 128).transpose(2, 0, 1)).reshape(128, NV * NCH)

    w_in = A(sc_w_in)
    Wc = w_in[:, :, 2048:4096].reshape(2, 16, 128, 16, 128)
    Wx = w_in[:, :, 4096:6144].reshape(2, 16, 128, 16, 128)
    w_sccx = np.ascontiguousarray(np.stack([Wc, Wx], axis=4).transpose(0, 3, 2, 1, 4, 5)).reshape(2, 16, 128, 4096)
    w_scb = np.stack([_cols256(w_in[j, :, 0:2048]) for j in range(2)])
    w_scout = np.stack([_cols256(A(sc_w_out)[j]) for j in range(2)])
    w_rgx = np.stack([_cols256(A(rg_w_x)[j]) for j in range(2)])
    w_rgg = np.stack([_cols256(A(rg_w_gate)[j]) for j in range(2)])
    w_rgout = np.stack([_cols256(A(rg_w_out)[j]) for j in range(2)])
    wa = A(rg_w_a).reshape(2, 8, 2, 128, 256).transpose(0, 1, 3, 2, 4).reshape(2, 8, 128, 512)
    wi = A(rg_w_i).reshape(2, 8, 2, 128, 256).transpose(0, 1, 3, 2, 4).reshape(2, 8, 128, 512)
    w_rgai = np.ascontiguousarray(np.concatenate([wa, wi], axis=3))
    Wg = A(ffn_w_gate).reshape(4, 16, 128, 44, 128)
    Wu = A(ffn_w_up).reshape(4, 16, 128, 44, 128)
    w_gu = np.ascontiguousarray(np.stack([Wg, Wu], axis=4).transpose(0, 3, 2, 1, 4, 5)).reshape(4, 44, 128, 4096)
    w_dn = np.ascontiguousarray(A(ffn_w_down).reshape(4, NQ, QCH, 128, 8, 256).transpose(0, 1, 4, 3, 2, 5)).reshape(
        4, NQ, 8, 128, QCH * 256)
    w_pg = np.stack([_cols256(A(ple_w_gate)[l]) for l in range(4)])
    w_pp = np.stack([_cols256(A(ple_w_proj)[l]) for l in range(4)])

    shared = dict(vecs=vecs, w_sccx=w_sccx, w_scb=w_scb, w_scout=w_scout, w_rgx=w_rgx, w_rgg=w_rgg,
                  w_rgai=w_rgai, w_rgout=w_rgout, w_gu=w_gu, w_dn=w_dn, w_pg=w_pg, w_pp=w_pp)

    in_maps = []
    for c in range(8):
        s, hf = c // 2, c % 2
        xt = np.concatenate([x_prompt[s, hf * TP:(hf + 1) * TP], x_sample[c * NS:(c + 1) * NS, 0]], axis=0)
        xT = np.ascontiguousarray(_feat_major(xt)).reshape(128, NCH * T)
        pt = np.concatenate([p_prompt[:, s, hf * TP:(hf + 1) * TP], p_sample[:, c * NS:(c + 1) * NS, 0]], axis=1)
        pTt = np.ascontiguousarray(pt.transpose(0, 2, 1).reshape(4, 2, 128, T).transpose(0, 2, 1, 3)).reshape(4, 128, 2 * T)
        sc = state_conv[:, c * NS:(c + 1) * NS]
        stc = np.ascontiguousarray(sc.reshape(2, NS, 2, NCH, 128).transpose(0, 4, 3, 2, 1)).reshape(2, 128, NCH * 2 * NS)
        sr = state_rg_conv[:, c * NS:(c + 1) * NS]
        strc = np.ascontiguousarray(sr.reshape(2, NS, 3, NCH, 128).transpose(0, 4, 3, 2, 1)).reshape(2, 128, NCH * 3 * NS)
        sh = state_rg_h[:, c * NS:(c + 1) * NS]
        sth = np.ascontiguousarray(sh.reshape(2, NS, NCH, 128).transpose(0, 3, 2, 1)).reshape(2, 128, NCH * NS)
        m = dict(shared)
        m.update(xT=xT, pT=pTt, st_conv=stc, st_rgc=strc, st_rgh=sth,
                 flag=np.full((128, 1), float(hf), f32))
        in_maps.append(m)
    return in_maps


def kernel(**inputs):
    in_maps = prepare(**inputs)
    nc = build_program()
    res = run_bass_kernel_spmd(nc, in_maps, core_ids=list(range(8)))
    return assemble(res.results)


def assemble(R, ncores=8):
    f32 = np.float32

    y_prompt = np.empty((4, 2048, D), f32)
    y_sample = np.empty((128, 1, D), f32)
    conv_p = np.empty((2, 4, 2, D), f32)
    conv_s = np.empty((2, 128, 2, D), f32)
    rgc_p = np.empty((2, 4, 3, D), f32)
    rgc_s = np.empty((2, 128, 3, D), f32)
    rgh_p = np.empty((2, 4, D), f32)
    rgh_s = np.empty((2, 128, D), f32)
    for c in range(ncores):
        s, hf = c // 2, c % 2
        y = np.asarray(R[c]["y"]).reshape(128, NCH, T).transpose(2, 1, 0).reshape(T, D)
        y_prompt[s, hf * TP:(hf + 1) * TP] = y[:TP]
        y_sample[c * NS:(c + 1) * NS, 0] = y[TP:]
        oc = np.asarray(R[c]["o_conv"]).reshape(2, 128, NCH, 2, 17).transpose(0, 4, 3, 2, 1).reshape(2, 17, 2, D)
        orc = np.asarray(R[c]["o_rgc"]).reshape(2, 128, NCH, 3, 17).transpose(0, 4, 3, 2, 1).reshape(2, 17, 3, D)
        oh = np.asarray(R[c]["o_rgh"]).reshape(2, 128, NCH, 17).transpose(0, 3, 2, 1).reshape(2, 17, D)
        conv_s[:, c * NS:(c + 1) * NS] = oc[:, 1:]
        rgc_s[:, c * NS:(c + 1) * NS] = orc[:, 1:]
        rgh_s[:, c * NS:(c + 1) * NS] = oh[:, 1:]
        if hf == 1:
            conv_p[:, s] = oc[:, 0]
            rgc_p[:, s] = orc[:, 0]
            rgh_p[:, s] = oh[:, 0]
    return (y_prompt, y_sample, conv_p, conv_s, rgc_p, rgc_s, rgh_p, rgh_s)
```

```python
import numpy as np
from contextlib import ExitStack
import concourse.bass as bass
import concourse.mybir as mybir
from concourse.bass_utils import run_bass_kernel_spmd

F32 = mybir.dt.float32
BF16 = mybir.dt.bfloat16
AF = mybir.ActivationFunctionType
COPY = AF.Identity
ALU = mybir.AluOpType

D = 2048
NCH = 16
TP = 1024
NS = 16
T = TP + NS
TILES = [(0, 347), (347, 347), (694, 346)]
PTILES = [(0, 347), (347, 347), (694, 330)]
DFF = 5632
NQ = 4
QCH = 11
EPS = 1e-6
NV = 35
SLOT = 4096
NSLOT = 3
DECLARED = []
DBG = set()

V_MIX, V_FFN, V_PLE, V_FIN, V_SCW, V_RGW, V_RGB, V_BA, V_BI, V_LAM = 0, 4, 8, 12, 13, 19, 27, 29, 31, 33


class Eng:
    def __init__(self, name, sem):
        self.name = name
        self.sem = sem
        self.cnt = 0
        self.seen = {}
        self.prog = []


class SemSrc:
    def __init__(self, name, sem):
        self.name = name
        self.sem = sem
        self.cnt = 0


class Buf:
    __slots__ = ("w", "r", "const", "consumed")

    def __init__(self, const=False):
        self.w = None
        self.r = []
        self.const = const
        self.consumed = True


class Prog:
    def __init__(self, nc, st):
        self.nc = nc
        self.st = st
        self.E = {}
        for n in ("pe", "act", "dve", "pool", "sp"):
            self.E[n] = Eng(n, st.enter_context(nc.semaphore("sem_" + n)))
        self.nsem = 0

    def semsrc(self, name):
        self.nsem += 1
        return SemSrc(name, self.st.enter_context(self.nc.semaphore("ds_%s_%d" % (name, self.nsem))))

    def _deps(self, E, R, W):
        deps = {}

        def add(d):
            if d is None:
                return
            src, tk = d
            if deps.get(src, 0) < tk:
                deps[src] = tk
        for b in R:
            add(b.w)
        for b in W:
            add(b.w)
            for d in b.r:
                add(d)
        waits = []
        for src, tk in deps.items():
            if src is E and E.name == "pe":
                continue
            if E.seen.get(src, 0) >= tk:
                continue
            E.seen[src] = tk
            waits.append((src.sem, tk))
        return waits

    def op(self, en, fn, R=(), W=(), tag=None):
        if tag is not None and ("skip:" + tag) in DBG:
            return
        E = self.E[en]
        waits = self._deps(E, R, W)
        E.cnt += 1
        my = (E, E.cnt)
        sem = E.sem

        def run(h, waits=waits, fn=fn, sem=sem):
            for s_, v in waits:
                h.wait_ge(s_, v)
            fn(h).then_inc(sem, 1)
        E.prog.append(run)
        for b in R:
            b.consumed = True
            if not b.const:
                b.r.append(my)
        for b in W:
            b.w = my
            b.r = []
            b.consumed = False

    def dma(self, qn, S, out, in_, R=(), W=()):
        Q = self.E[qn]
        waits = self._deps(Q, R, W)
        if S.cnt > 0 and Q.seen.get(S, 0) < S.cnt:
            Q.seen[S] = S.cnt
            waits.append((S.sem, S.cnt))
        S.cnt += 16
        my = (S, S.cnt)
        sem = S.sem

        def run(h, waits=waits, sem=sem, out=out, in_=in_):
            for s_, v in waits:
                h.wait_ge(s_, v)
            h.dma_start(out=out, in_=in_).then_inc(sem, 16)
        Q.prog.append(run)
        for b in R:
            if not b.const:
                b.r.append(my)
        for b in W:
            b.w = my
            b.r = []

    def wait_all(self, en, srcs):
        E = self.E[en]
        waits = [(s.sem, s.cnt) for s in srcs if s.cnt > 0]

        def run(h, waits=waits):
            for s_, v in waits:
                h.wait_ge(s_, v)
        E.prog.append(run)


def _split_cols(n):
    for b in (2048, 1408, 1040, 1024, 512):
        if n % b == 0 and b <= n:
            return b
    raise ValueError(n)


def build_program(n_cores=8, stop=None):
    del DECLARED[:]
    nc = bass.Bass("TRN2", target_bir_lowering=False)

    def din(name, shape):
        return nc.dram_tensor(name, list(shape), F32, kind="ExternalInput").ap()

    def dout(name, shape):
        return nc.dram_tensor(name, list(shape), F32, kind="ExternalOutput").ap()

    xT = din("xT", [128, NCH * T])
    pT = din("pT", [4, 128, 2 * T])
    vecs = din("vecs", [128, NV * NCH])
    st_conv = din("st_conv", [2, 128, NCH * 2 * NS])
    st_rgc = din("st_rgc", [2, 128, NCH * 3 * NS])
    st_rgh = din("st_rgh", [2, 128, NCH * NS])
    flag_d = din("flag", [128, 1])
    WSHAPES = {
        "w_sccx": [2, 16, 128, 4096],
        "w_scb": [2, 8, 128, 4096],
        "w_scout": [2, 8, 128, 4096],
        "w_rgx": [2, 8, 128, 4096],
        "w_rgg": [2, 8, 128, 4096],
        "w_rgai": [2, 8, 128, 1024],
        "w_rgout": [2, 8, 128, 4096],
        "w_gu": [4, 44, 128, 4096],
        "w_dn": [4, NQ, 8, 128, QCH * 256],
        "w_pg": [4, 8, 128, 4096],
        "w_pp": [4, 8, 128, 512],
    }
    wcache = {}

    def wd(name):
        if name not in wcache:
            wcache[name] = din(name, WSHAPES[name])
            DECLARED.append(name)
        return wcache[name]

    y_d = dout("y", [128, NCH * T])
    o_conv = dout("o_conv", [2, 128, NCH * 2 * 17])
    o_rgc = dout("o_rgc", [2, 128, NCH * 3 * 17])
    o_rgh = dout("o_rgh", [2, 128, NCH * 17])

    ccsrc = [nc.dram_tensor("ccsrc%d" % l, [128, NCH * 4], F32, kind="Internal").ap() for l in range(4)]
    ccdst = [nc.dram_tensor("ccdst%d" % l, [256, NCH * 4], F32, kind="Internal", addr_space="Local").ap()
             for l in range(4)]

    with ExitStack() as st:
        def sb(name, shape, dt=F32):
            return st.enter_context(nc.sbuf_tensor(name, list(shape), dt))

        X = sb("X", [128, NCH, T])
        HN = sb("HN", [128, NCH, T], BF16)
        Z = sb("Z", [128, NCH, T], BF16)
        RING = [sb("RING%d" % i, [128, SLOT], BF16) for i in range(NSLOT)]
        VEC = sb("VEC", [128, NV, NCH])
        NLC = sb("NLC", [128, 2, NCH])
        HBV = sb("HBV", [128, 2, 2, NCH])
        STC = sb("STC", [128, NCH, 2, NS])
        STR = sb("STR", [128, NCH, 3, NS])
        STH = sb("STH", [128, NCH, NS])
        OSC = sb("OSC", [128, NCH, 2, 17])
        OSR = sb("OSR", [128, NCH, 3, 17])
        OSH = sb("OSH", [128, NCH, 17])
        ONES = sb("ONES", [128, 128], BF16)
        EXS = sb("EXS", [128, NCH, 4])
        EXR = sb("EXR", [128, NCH, 4])
        HALO = sb("HALO", [128, NCH, 4])
        YS = sb("YS", [128, NCH, 2])
        BS = sb("BS", [128, NCH, 2])
        FT = sb("FT", [128, 4, NCH])
        FLAG = sb("FLAG", [128, 1])
        UX = sb("UX", [128, 3 + T])
        U = [sb("U%d" % i, [128, T]) for i in range(2)]
        UB = sb("UB", [128, 2, T], BF16)
        Rb = sb("Rb", [128, T])
        Ib = sb("Ib", [128, T])
        Mb = sb("Mb", [128, T])
        Gt = sb("Gt", [128, 2, 512])
        PS = [st.enter_context(nc.psum_tensor("PS%d" % i, [128, 512], F32)) for i in range(8)]

        P = Prog(nc, st)
        op, dma = P.op, P.dma

        bX = [[Buf() for _ in TILES] for _ in range(NCH)]
        bHN = [[Buf() for _ in TILES] for _ in range(NCH)]
        bZ = [[Buf() for _ in TILES] for _ in range(NCH)]
        bRING = [Buf() for _ in range(NSLOT)]
        bVEC, bNLC, bONES, bFLAG = Buf(True), Buf(True), Buf(True), Buf(True)
        bSTC, bSTR, bSTH = Buf(), Buf(), Buf()
        bOSC, bOSR, bOSH = Buf(), Buf(), Buf()
        bEXS, bEXR, bHALO, bYS, bBS, bFT = Buf(), Buf(), Buf(), Buf(), Buf(), Buf()
        bUX = Buf()
        bU = [[Buf() for _ in TILES] for _ in range(2)]
        bUB = [[Buf() for _ in TILES] for _ in range(2)]
        bR = [Buf() for _ in TILES]
        bI = [Buf() for _ in TILES]
        bM = [Buf() for _ in TILES]
        bG = [Buf(), Buf()]
        bPS = [Buf() for _ in range(8)]
        bCS = [Buf() for _ in range(4)]
        bCD = [Buf() for _ in range(4)]
        bY = Buf()

        ring_sem = [P.semsrc("ring%d" % i) for i in range(NSLOT)]
        misc_sems = {"sp": [P.semsrc("msp%d" % i) for i in range(4)],
                     "pool": [P.semsrc("mpl%d" % i) for i in range(4)]}
        out_sems = []
        state = {"bank": 0, "slot": 0, "misc": 0, "g": 0}

        def bank():
            b = state["bank"]
            state["bank"] = (b + 1) % 8
            assert bPS[b].consumed, "PSUM bank %d re-allocated before its reader was emitted" % b
            return b

        def msem(q="sp"):
            s = misc_sems[q][state["misc"] % 4]
            state["misc"] += 1
            return s

        def load_slab(src2d, nelem, avoid=None):
            s = state["slot"]
            if s == avoid:
                s = (s + 1) % NSLOT
            state["slot"] = (s + 1) % NSLOT
            b = _split_cols(nelem)
            dma("pool", ring_sem[s],
                RING[s][:, 0:nelem].rearrange("p (a b) -> p a b", b=b),
                src2d.rearrange("p (a b) -> p a b", b=b), W=[bRING[s]])
            return s

        def mm_group(pbank, ncols, lhs_list, rhs_list, R, col0=0):
            n = len(lhs_list)

            def fn(h):
                ins = None
                for k in range(n):
                    ins = h.matmul(PS[pbank][:, col0:col0 + ncols], lhsT=lhs_list[k], rhs=rhs_list[k],
                                   start=(k == 0), stop=(k == n - 1))
                return ins
            op("pe", fn, R=R, W=[bPS[pbank]])

        def vcol(idx, c):
            return VEC[:, idx, c:c + 1]

        dma("sp", msem(), VEC[:].rearrange("p a b -> p (a b)"), vecs, W=[bVEC])
        dma("sp", msem(), FLAG[:], flag_d, W=[bFLAG])
        sx = P.semsrc("xload")
        for c in range(NCH):
            pass
        xT3 = xT.rearrange("p (a b) -> p a b", b=T)
        for ti, (t0, tn) in enumerate(TILES):
            sx = P.semsrc("xload%d" % ti)
            dma("sp", sx, X[:, :, t0:t0 + tn], xT3[:, :, t0:t0 + tn], W=[bX[c][ti] for c in range(NCH)])
        op("dve", lambda h: h.memset(ONES[:], 1.0), W=[bONES])
        op("dve", lambda h: h.memset(EXS[:].rearrange("p a b -> p (a b)"), 0.0), W=[bEXS])
        for j in range(2):
            op("act", lambda h, j=j: h.activation(out=NLC[:, j, :], in_=VEC[:, V_LAM + j, :], func=AF.Exp, scale=-1.0),
               R=[bVEC], W=[bNLC])
            op("act", lambda h, j=j: h.activation(out=NLC[:, j, :], in_=NLC[:, j, :], func=AF.Ln, bias=1.0),
               R=[bNLC], W=[bNLC])
            op("dve", lambda h, j=j: h.tensor_scalar(out=NLC[:, j, :], in0=NLC[:, j, :], scalar1=-4.0, scalar2=None,
                                                     op0=ALU.mult), R=[bNLC], W=[bNLC])

        for j in range(2):
            for q_, vi in ((0, V_BA), (1, V_BI)):
                op("dve", lambda h, j=j, q_=q_, vi=vi: h.tensor_scalar(out=HBV[:, j, q_, :], in0=VEC[:, vi + j, :], scalar1=0.5,
                                                                       scalar2=None, op0=ALU.mult), R=[bVEC], W=[bVEC])
        def norm(gidx, dst="HN"):
            for ti, (t0, tn) in enumerate(TILES):
                pb = bank()
                for c in range(NCH):
                    q = c % 4
                    h_, tq = q // 2, q % 2
                    sqap = UB[:, h_, TILES[tq][0]: TILES[tq][0] + tn]
                    op("act", lambda h, c=c, sqap=sqap, t0=t0, tn=tn: h.activation(
                        out=sqap, in_=X[:, c, t0:t0 + tn], func=AF.Square),
                        R=[bX[c][ti]], W=[bUB[h_][tq]])

                    def fn(h, c=c, sqap=sqap, pb=pb, tn=tn):
                        return h.matmul(PS[pb][:, 0:tn], lhsT=ONES[:], rhs=sqap, start=(c == 0), stop=(c == NCH - 1))
                    op("pe", fn, R=[bUB[h_][tq], bONES], W=[bPS[pb]] if c == 0 else [bPS[pb]])
                rs = Gt[:, 0, 0:tn]
                op("act", lambda h, pb=pb, tn=tn, rs=rs: h.activation(out=rs, in_=PS[pb][:, 0:tn], func=AF.Sqrt,
                                                                       scale=1.0 / D, bias=EPS),
                   R=[bPS[pb]], W=[bG[0]])
                op("dve", lambda h, rs=rs: h.reciprocal(out=rs, in_=rs), R=[bG[0]], W=[bG[0]])
                for c in range(NCH):
                    if dst == "HN":
                        o = HN[:, c, t0:t0 + tn]
                        Wb = [bHN[c][ti]]
                    else:
                        o = X[:, c, t0:t0 + tn]
                        Wb = [bX[c][ti]]
                    op("dve", lambda h, c=c, o=o, t0=t0, tn=tn, rs=rs: h.scalar_tensor_tensor(
                        out=o, in0=X[:, c, t0:t0 + tn], scalar=vcol(gidx, c), in1=rs, op0=ALU.mult, op1=ALU.mult),
                        R=[bX[c][ti], bG[0], bVEC], W=Wb)

        def out_proj(wd, nk, zb, pre=(), defer_t0=0):
            if "nooutproj" in DBG:
                return

            def grp(s, n, cc, ti):
                ch = 2 * n + cc
                t0, tn = TILES[ti]
                pb = bank()
                lhs = [RING[s][:, kc * 256 + cc * 128: kc * 256 + cc * 128 + 128] for kc in range(nk)]
                rhs = [Z[:, kc, t0:t0 + tn] for kc in range(nk)]
                mm_group(pb, tn, lhs, rhs, R=[bRING[s]] + [zb[kc][ti] for kc in range(nk)])
                op("dve", lambda h, ch=ch, t0=t0, tn=tn, pb=pb: h.tensor_tensor(
                    out=X[:, ch, t0:t0 + tn], in0=PS[pb][:, 0:tn], in1=X[:, ch, t0:t0 + tn], op=ALU.add),
                    R=[bPS[pb]], W=[bX[ch][ti]])
            assert defer_t0 <= len(pre)
            for ti_set in ((1, 2), (0,)):
                for n in range(defer_t0):
                    for cc in range(2):
                        for ti in ti_set:
                            grp(pre[n], n, cc, ti)
            for n in range(defer_t0, 8):
                s = pre[n] if n < len(pre) else load_slab(wd[n], nk * 256)
                for cc in range(2):
                    for ti in (0, 1, 2):
                        grp(s, n, cc, ti)

        def exchange(l):
            if "noexch" in DBG:
                op("dve", lambda h: h.memset(HALO[:].rearrange("p a b -> p (a b)"), 0.0), W=[bHALO])
                return
            S1 = msem("pool")
            dma("pool", S1, ccsrc[l], EXS[:].rearrange("p a b -> p (a b)"), R=[bEXS], W=[bCS[l]])
            op("pool", lambda h, l=l: h.collective_compute(
                "AllGather", ALU.bypass, replica_groups=[[2 * i, 2 * i + 1] for i in range(n_cores // 2)],
                ins=[ccsrc[l]], outs=[ccdst[l]]), R=[bCS[l]], W=[bCD[l]])
            S2 = msem("pool")
            dma("pool", S2, EXR[:].rearrange("p a b -> p (a b)"), ccdst[l][0:128, :], R=[bCD[l]], W=[bEXR])
            op("dve", lambda h: h.tensor_scalar(out=HALO[:].rearrange("p a b -> p (a b)"),
                                                in0=EXR[:].rearrange("p a b -> p (a b)"),
                                                scalar1=FLAG[:, 0:1], scalar2=None, op0=ALU.mult),
               R=[bEXR, bFLAG], W=[bHALO])

        def conv_mixer(l):
            j = l // 2
            iw = V_SCW + 3 * j
            dma("sp", msem(), STC[:].rearrange("p a b c -> p (a b c)"), st_conv[j], W=[bSTC])
            CX = UX
            op("dve", lambda h: h.memset(CX[:, 0:2], 0.0), W=[bUX])
            sb_ = None
            for f in range(NCH if "nf" not in DBG else 2):
                s_cx = load_slab(wd("w_sccx")[j, f], 4096)
                if f % 2 == 0:
                    sb_ = load_slab(wd("w_scb")[j, f // 2], 4096)
                cc = f % 2
                pcs, pxs, pbks = {}, {}, {}

                def pe_b(ti):
                    t0p, tnp = TILES[ti]
                    pbks[ti] = bank()
                    mm_group(pbks[ti], tnp, [RING[sb_][:, kc * 256 + cc * 128: kc * 256 + cc * 128 + 128]
                                             for kc in range(NCH)], [HN[:, kc, t0p:t0p + tnp] for kc in range(NCH)],
                             R=[bRING[sb_]] + [bHN[kc][ti] for kc in range(NCH)])

                def ew_A(ti):
                    t0, tn = TILES[ti]
                    pc, px = pcs[ti], pxs[ti]
                    csb = U[0][:, t0:t0 + tn]
                    tmp = U[1][:, t0:t0 + tn]
                    op("act", lambda h, csb=csb, pc=pc, tn=tn: h.activation(out=csb, in_=PS[pc][:, 0:tn], func=COPY),
                       R=[bPS[pc]], W=[bU[0][ti]], tag="evac")
                    cxo = CX[:, 2 + t0: 2 + t0 + tn]
                    op("dve", lambda h, cxo=cxo, csb=csb, px=px, tn=tn: h.tensor_tensor(
                        out=cxo, in0=PS[px][:, 0:tn], in1=csb, op=ALU.mult),
                        R=[bPS[px], bU[0][ti]], W=[bUX], tag="cx")
                    pn = min(tn, TP - t0)
                    has_s = t0 + tn > TP
                    ptmp = U[1][:, t0:t0 + pn]
                    op("dve", lambda h, ptmp=ptmp, t0=t0, pn=pn, f=f: h.tensor_scalar(
                        out=ptmp, in0=CX[:, t0:t0 + pn], scalar1=vcol(iw, f), scalar2=None, op0=ALU.mult),
                        R=[bUX, bVEC], W=[bU[1][ti]], tag="conv")
                    for k in (1, 2):
                        op("dve", lambda h, ptmp=ptmp, t0=t0, pn=pn, f=f, k=k: h.scalar_tensor_tensor(
                            out=ptmp, in0=CX[:, t0 + k:t0 + k + pn], scalar=vcol(iw + k, f), in1=ptmp,
                            op0=ALU.mult, op1=ALU.add), R=[bUX, bU[1][ti], bVEC], W=[bU[1][ti]], tag="conv")
                    if has_s:
                        stmp = U[1][:, TP:T]
                        cxs = CX[:, 2 + TP:2 + T]
                        op("dve", lambda h, stmp=stmp, f=f: h.tensor_scalar(
                            out=stmp, in0=STC[:, f, 0, :], scalar1=vcol(iw, f), scalar2=None, op0=ALU.mult),
                            R=[bSTC, bVEC], W=[bU[1][ti]], tag="conv")
                        op("dve", lambda h, stmp=stmp, f=f: h.scalar_tensor_tensor(
                            out=stmp, in0=STC[:, f, 1, :], scalar=vcol(iw + 1, f), in1=stmp, op0=ALU.mult, op1=ALU.add),
                            R=[bSTC, bU[1][ti], bVEC], W=[bU[1][ti]], tag="conv")
                        op("dve", lambda h, stmp=stmp, f=f, cxs=cxs: h.scalar_tensor_tensor(
                            out=stmp, in0=cxs, scalar=vcol(iw + 2, f), in1=stmp, op0=ALU.mult, op1=ALU.add),
                            R=[bUX, bU[1][ti], bVEC], W=[bU[1][ti]], tag="conv")
                        op("act", lambda h, f=f: h.activation(out=OSC[:, f, 0, 1:17], in_=STC[:, f, 1, :], func=COPY),
                           R=[bSTC], W=[bOSC], tag="small1")
                        op("act", lambda h, f=f, cxs=cxs: h.activation(out=OSC[:, f, 1, 1:17], in_=cxs, func=COPY),
                           R=[bUX], W=[bOSC], tag="small1")
                    if has_s:
                        op("dve", lambda h, f=f: h.tensor_copy(out=EXS[:, f, 0:2], in_=CX[:, TP:TP + 2]),
                           R=[bUX], W=[bEXS], tag="small2")
                        op("dve", lambda h, f=f: h.tensor_copy(out=OSC[:, f, :, 0], in_=CX[:, TP:TP + 2]),
                           R=[bUX], W=[bOSC], tag="small3")

                def ew_B(ti):
                    t0, tn = TILES[ti]
                    pbk = pbks[ti]
                    tmp = U[1][:, t0:t0 + tn]
                    if ti == 0:
                        op("dve", lambda h, f=f: h.tensor_copy(out=YS[:, f, :], in_=U[1][:, 0:2]),
                           R=[bU[1][ti]], W=[bYS], tag="small2")
                        op("dve", lambda h, f=f, pbk=pbk: h.tensor_copy(out=BS[:, f, :], in_=PS[pbk][:, 0:2]),
                           R=[bPS[pbk]], W=[bBS], tag="small2")
                    op("dve", lambda h, f=f, t0=t0, tn=tn, tmp=tmp, pbk=pbk: h.tensor_tensor(
                        out=Z[:, f, t0:t0 + tn], in0=PS[pbk][:, 0:tn], in1=tmp, op=ALU.mult),
                        R=[bPS[pbk], bU[1][ti]], W=[bZ[f][ti]], tag="z")


                for ti, (t0, tn) in enumerate(TILES):
                    pcs[ti], pxs[ti] = bank(), bank()
                    hn = [HN[:, kc, t0:t0 + tn] for kc in range(NCH)]
                    Rh = [bHN[kc][ti] for kc in range(NCH)]
                    mm_group(pcs[ti], tn, [RING[s_cx][:, kc * 256: kc * 256 + 128] for kc in range(NCH)], hn,
                             R=[bRING[s_cx]] + Rh)
                    mm_group(pxs[ti], tn, [RING[s_cx][:, kc * 256 + 128: kc * 256 + 256] for kc in range(NCH)], hn,
                             R=[bRING[s_cx]] + Rh)
                    ew_A(ti)
                    if ti > 0:
                        pe_b(ti - 1)
                        ew_B(ti - 1)
                pe_b(2)
                ew_B(2)
            so = P.semsrc("oc%d" % j)
            out_sems.append(so)
            dma("sp", so, o_conv[j], OSC[:].rearrange("p a b c -> p (a b c)"), R=[bOSC])
            pre = [load_slab(wd("w_scout")[j][n_], NCH * 256) for n_ in range(2)]
            exchange(l)
            w0 = VEC[:, iw, :]
            w1 = VEC[:, iw + 1, :]
            h0 = HALO[:, :, 0]
            h1 = HALO[:, :, 1]
            op("dve", lambda h: h.tensor_tensor(out=FT[:, 0, :], in0=w0, in1=h0, op=ALU.mult), R=[bHALO, bVEC], W=[bFT])
            op("dve", lambda h: h.tensor_tensor(out=FT[:, 1, :], in0=w1, in1=h1, op=ALU.mult), R=[bHALO, bVEC, bFT], W=[bFT])
            op("dve", lambda h: h.tensor_tensor(out=FT[:, 0, :], in0=FT[:, 0, :], in1=FT[:, 1, :], op=ALU.add), R=[bFT], W=[bFT])
            op("dve", lambda h: h.tensor_tensor(out=FT[:, 2, :], in0=w0, in1=h1, op=ALU.mult), R=[bHALO, bVEC, bFT], W=[bFT])
            op("dve", lambda h: h.tensor_tensor(out=YS[:, :, 0], in0=YS[:, :, 0], in1=FT[:, 0, :], op=ALU.add), R=[bFT, bYS], W=[bYS])
            op("dve", lambda h: h.tensor_tensor(out=YS[:, :, 1], in0=YS[:, :, 1], in1=FT[:, 2, :], op=ALU.add), R=[bFT, bYS], W=[bYS])
            op("dve", lambda h: h.tensor_tensor(out=Z[:, :, 0:2], in0=YS[:], in1=BS[:], op=ALU.mult),
               R=[bYS, bBS], W=[bZ[f][0] for f in range(NCH)])
            out_proj(wd("w_scout")[j], NCH, bZ, pre=pre, defer_t0=2)

        def rg_pass(l, final, pre=None):
            j = l // 2
            iw = V_RGW + 4 * j
            ib = V_RGB + j
            tiles = list(enumerate(TILES)) if final else list(enumerate(TILES))[:2]
            for n in range(8):
                if n == 0 and pre is not None:
                    s_x, s_ai, s_g = pre
                else:
                    s_x = load_slab(wd("w_rgx")[j, n], 4096)
                    s_ai = load_slab(wd("w_rgai")[j, n], 1024)
                    s_g = load_slab(wd("w_rgg")[j, n], 4096) if final else None
                for cc in range(2):
                    f = 2 * n + cc
                    if final:
                        op("dve", lambda h, f=f: h.tensor_copy(out=UX[:, 0:3], in_=HALO[:, f, 0:3]),
                           R=[bHALO], W=[bUX])
                    else:
                        op("dve", lambda h: h.memset(UX[:, 0:3], 0.0), W=[bUX])
                    pbs, pgs = {}, {}
                    hnr = lambda t0, tn: [HN[:, kc, t0:t0 + tn] for kc in range(NCH)]
                    for ti, (t0, tn) in tiles:
                        pbs[ti] = bank()
                        mm_group(pbs[ti], tn, [RING[s_x][:, kc * 256 + cc * 128: kc * 256 + cc * 128 + 128] for kc in range(NCH)],
                                 hnr(t0, tn), R=[bRING[s_x]] + [bHN[kc][ti] for kc in range(NCH)])
                    for ti, (t0, tn) in tiles:
                        op("act", lambda h, t0=t0, tn=tn, pb=pbs[ti]: h.activation(out=UX[:, 3 + t0:3 + t0 + tn], in_=PS[pb][:, 0:tn], func=COPY),
                           R=[bPS[pbs[ti]]], W=[bUX])
                    bUall = [bU[cc][0], bU[cc][1], bU[cc][2]]
                    op("act", lambda h, cc=cc, f=f: h.activation(out=U[cc][:, 0:TP], in_=UX[:, 0:TP], func=COPY,
                                                                  scale=vcol(iw, f), bias=vcol(ib, f)),
                       R=[bUX, bVEC], W=bUall)
                    for k in (1, 2, 3):
                        op("dve", lambda h, cc=cc, f=f, k=k: h.scalar_tensor_tensor(
                            out=U[cc][:, 0:TP], in0=UX[:, k:k + TP], scalar=vcol(iw + k, f), in1=U[cc][:, 0:TP],
                            op0=ALU.mult, op1=ALU.add), R=[bUX, bVEC] + bUall, W=bUall)
                    for ti, (t0, tn) in tiles:
                        if t0 + tn > TP:
                            uo = U[cc][:, TP:T]
                            uxo = UX[:, 3 + TP:3 + T]
                            op("dve", lambda h, uo=uo, f=f: h.tensor_scalar(
                                out=uo, in0=STR[:, f, 0, :], scalar1=vcol(iw, f), scalar2=vcol(ib, f),
                                op0=ALU.mult, op1=ALU.add), R=[bSTR, bVEC], W=[bU[cc][ti]])
                            for k in (1, 2):
                                op("dve", lambda h, uo=uo, f=f, k=k: h.scalar_tensor_tensor(
                                    out=uo, in0=STR[:, f, k, :], scalar=vcol(iw + k, f), in1=uo,
                                    op0=ALU.mult, op1=ALU.add), R=[bSTR, bU[cc][ti], bVEC], W=[bU[cc][ti]])
                            op("dve", lambda h, uo=uo, f=f, uxo=uxo: h.scalar_tensor_tensor(
                                out=uo, in0=uxo, scalar=vcol(iw + 3, f), in1=uo, op0=ALU.mult, op1=ALU.add),
                                R=[bUX, bU[cc][ti], bVEC], W=[bU[cc][ti]])
                            op("act", lambda h, f=f: h.activation(out=OSR[:, f, 0:2, 1:17], in_=STR[:, f, 1:3, :], func=COPY),
                               R=[bSTR], W=[bOSR])
                            op("act", lambda h, f=f, uxo=uxo: h.activation(out=OSR[:, f, 2, 1:17], in_=uxo, func=COPY),
                               R=[bUX], W=[bOSR])
                    op("act", lambda h, cc=cc: h.activation(out=UB[:, cc, 0:T], in_=U[cc][:, 0:T], func=COPY),
                       R=bUall, W=[bUB[cc][0], bUB[cc][1], bUB[cc][2]])
                    if final:
                        op("dve", lambda h, f=f: h.tensor_copy(out=OSR[:, f, :, 0], in_=UX[:, TP:TP + 3]), R=[bUX], W=[bOSR])
                    else:
                        op("dve", lambda h, f=f: h.tensor_copy(out=EXS[:, f, 0:3], in_=UX[:, TP:TP + 3]), R=[bUX], W=[bEXS])
                if final:
                    for cc in range(2):
                        f = 2 * n + cc
                        pgs = {}
                        for ti, (t0, tn) in tiles:
                            pgs[ti] = bank()
                            mm_group(pgs[ti], tn, [RING[s_g][:, kc * 256 + cc * 128: kc * 256 + cc * 128 + 128] for kc in range(NCH)],
                                     [HN[:, kc, t0:t0 + tn] for kc in range(NCH)], R=[bRING[s_g]] + [bHN[kc][ti] for kc in range(NCH)])
                        for ti, (t0, tn) in tiles:
                            op("act", lambda h, f=f, t0=t0, tn=tn, pg=pgs[ti]: h.activation(out=Z[:, f, t0:t0 + tn], in_=PS[pg][:, 0:tn], func=AF.Gelu_apprx_tanh),
                               R=[bPS[pgs[ti]], bHALO], W=[bZ[f][ti]])
                for oc in range(2):
                    f = 2 * n + oc
                    prs, pis = {}, {}
                    for ti, (t0, tn) in tiles:
                        prs[ti], pis[ti] = bank(), bank()
                        rhs = [UB[:, kc, t0:t0 + tn] for kc in range(2)]
                        Ru = [bRING[s_ai], bUB[0][ti], bUB[1][ti]]
                        mm_group(prs[ti], tn, [RING[s_ai][:, kc * 256 + oc * 128: kc * 256 + oc * 128 + 128] for kc in range(2)], rhs, R=Ru)
                        mm_group(pis[ti], tn, [RING[s_ai][:, 512 + kc * 256 + oc * 128: 512 + kc * 256 + oc * 128 + 128] for kc in range(2)], rhs, R=Ru)
                    sl = lambda B_, t0, tn: B_[:, t0:t0 + tn]
                    for ti, (t0, tn) in tiles:
                        op("act", lambda h, t0=t0, tn=tn, pr=prs[ti], f=f: h.activation(out=sl(Rb, t0, tn), in_=PS[pr][:, 0:tn], func=AF.Tanh,
                                                                                      scale=0.5, bias=HBV[:, j, 0, f:f + 1]),
                           R=[bPS[prs[ti]], bVEC], W=[bR[ti]])
                    for ti, (t0, tn) in tiles:
                        op("act", lambda h, t0=t0, tn=tn, pi=pis[ti], f=f: h.activation(out=sl(Ib, t0, tn), in_=PS[pi][:, 0:tn], func=AF.Tanh,
                                                                                      scale=0.5, bias=HBV[:, j, 1, f:f + 1]),
                           R=[bPS[pis[ti]], bVEC], W=[bI[ti]])
                    bRa, bIa, bMa = list(bR), list(bI), list(bM)
                    op("act", lambda h, f=f: h.activation(out=Rb[:, 0:T], in_=Rb[:, 0:T], func=AF.Exp,
                                                          scale=NLC[:, j, f:f + 1], bias=NLC[:, j, f:f + 1]),
                       R=bRa + [bNLC], W=bRa)
                    op("act", lambda h: h.activation(out=Mb[:, 0:T], in_=Rb[:, 0:T], func=AF.Square), R=bRa, W=bMa)
                    op("act", lambda h: h.activation(out=Mb[:, 0:T], in_=Mb[:, 0:T], func=AF.Sqrt, scale=-0.25, bias=0.25), R=bMa, W=bMa)
                    op("dve", lambda h: h.scalar_tensor_tensor(out=Ib[:, 0:T], in0=Ib[:, 0:T], scalar=1.0, in1=Mb[:, 0:T],
                                                               op0=ALU.add, op1=ALU.mult), R=bIa + bMa, W=bIa)
                    op("dve", lambda h, oc=oc: h.tensor_tensor(out=Ib[:, 0:T], in0=Ib[:, 0:T], in1=U[oc][:, 0:T], op=ALU.mult),
                       R=bIa + [bU[oc][0], bU[oc][1], bU[oc][2]], W=bIa)
                    init = HALO[:, f, 3:4] if final else 0.0
                    op("dve", lambda h, init=init: h.tensor_tensor_scan(out=Mb[:, 0:TP], data0=Rb[:, 0:TP], data1=Ib[:, 0:TP],
                                                                         initial=init, op0=ALU.mult, op1=ALU.add),
                       R=[bR[0], bR[1], bR[2], bI[0], bI[1], bI[2], bHALO], W=[bM[0], bM[1], bM[2]])
                    if not final:
                        op("dve", lambda h, f=f: h.tensor_copy(out=EXS[:, f, 3:4], in_=Mb[:, TP - 1:TP]),
                           R=[bM[2]], W=[bEXS])
                        continue
                    op("dve", lambda h, f=f: h.tensor_copy(out=OSH[:, f, 0:1], in_=Mb[:, TP - 1:TP]), R=[bM[2]], W=[bOSH])
                    op("dve", lambda h, f=f: h.tensor_tensor(out=Mb[:, TP:T], in0=Rb[:, TP:T], in1=STH[:, f, :], op=ALU.mult),
                       R=[bR[2], bSTH], W=[bM[2]])
                    op("dve", lambda h: h.tensor_tensor(out=Mb[:, TP:T], in0=Mb[:, TP:T], in1=Ib[:, TP:T], op=ALU.add),
                       R=[bM[2], bI[2]], W=[bM[2]])
                    op("act", lambda h, f=f: h.activation(out=OSH[:, f, 1:17], in_=Mb[:, TP:T], func=COPY), R=[bM[2]], W=[bOSH])
                    op("dve", lambda h, f=f: h.tensor_tensor(out=Z[:, f, 0:T], in0=Z[:, f, 0:T], in1=Mb[:, 0:T], op=ALU.mult),
                       R=bMa + bZ[f], W=bZ[f])

        Zf = Z[:].bitcast(F32)

        def zview(c0, c1):
            return Zf[:, c0:c1, :].rearrange("p a b -> p (a b)")
        SETS = [
            dict(UX=UX[:], U=[U[0][:], U[1][:]], UB=UB[:], R=Rb[:], I=Ib[:], M=Mb[:],
                 bUX=bUX, bU=bU, bUB=bUB, bR=bR, bI=bI, bM=bM),
            dict(UX=zview(0, 3), U=[zview(3, 5), zview(5, 7)], UB=Z[:, 13:15, :], R=zview(7, 9), I=zview(9, 11), M=zview(11, 13),
                 bUX=Buf(), bU=[[Buf() for _ in TILES] for _ in range(2)], bUB=[[Buf() for _ in TILES] for _ in range(2)],
                 bR=[Buf() for _ in TILES], bI=[Buf() for _ in TILES], bM=[Buf() for _ in TILES]),
        ]

        def rg_block1(l, n, S, s_ai, aoff):
            j = l // 2
            iw = V_RGW + 4 * j
            ib = V_RGB + j
            tiles = list(enumerate(PTILES))
            UXs, Us, UBs, Rs, Is, Ms = S["UX"], S["U"], S["UB"], S["R"], S["I"], S["M"]
            s_x = load_slab(wd("w_rgx")[j, n], 4096, avoid=s_ai)
            for cc in range(2):
                f = 2 * n + cc
                op("dve", lambda h: h.memset(UXs[:, 0:3], 0.0), W=[S["bUX"]])
                pbs = {}
                for ti, (t0, tn) in tiles:
                    pbs[ti] = bank()
                    mm_group(pbs[ti], tn, [RING[s_x][:, kc * 256 + cc * 128: kc * 256 + cc * 128 + 128] for kc in range(NCH)],
                             [HN[:, kc, t0:t0 + tn] for kc in range(NCH)], R=[bRING[s_x]] + [bHN[kc][ti] for kc in range(NCH)])
                yield
                for ti, (t0, tn) in tiles:
                    op("act", lambda h, t0=t0, tn=tn, pb=pbs[ti]: h.activation(out=UXs[:, 3 + t0:3 + t0 + tn], in_=PS[pb][:, 0:tn], func=COPY),
                       R=[bPS[pbs[ti]]], W=[S["bUX"]])
                yield
                bUall = list(S["bU"][cc])
                op("act", lambda h, cc=cc, f=f: h.activation(out=Us[cc][:, 0:TP], in_=UXs[:, 0:TP], func=COPY,
                                                              scale=vcol(iw, f), bias=vcol(ib, f)),
                   R=[S["bUX"], bVEC], W=bUall)
                yield
                for k in (1, 2, 3):
                    op("dve", lambda h, cc=cc, f=f, k=k: h.scalar_tensor_tensor(
                        out=Us[cc][:, 0:TP], in0=UXs[:, k:k + TP], scalar=vcol(iw + k, f), in1=Us[cc][:, 0:TP],
                        op0=ALU.mult, op1=ALU.add), R=[S["bUX"], bVEC] + bUall, W=bUall)
                yield
                op("act", lambda h, cc=cc: h.activation(out=UBs[:, cc, 0:TP], in_=Us[cc][:, 0:TP], func=COPY),
                   R=bUall, W=list(S["bUB"][cc]))
                op("dve", lambda h, f=f: h.tensor_copy(out=EXS[:, f, 0:3], in_=UXs[:, TP:TP + 3]), R=[S["bUX"]], W=[bEXS])
                yield
            for oc in range(2):
                f = 2 * n + oc
                prs, pis = {}, {}
                for ti, (t0, tn) in tiles:
                    prs[ti], pis[ti] = bank(), bank()
                    rhs = [UBs[:, kc, t0:t0 + tn] for kc in range(2)]
                    Ru = [bRING[s_ai], S["bUB"][0][ti], S["bUB"][1][ti]]
                    mm_group(prs[ti], tn, [RING[s_ai][:, aoff + kc * 256 + oc * 128: aoff + kc * 256 + oc * 128 + 128] for kc in range(2)], rhs, R=Ru)
                    mm_group(pis[ti], tn, [RING[s_ai][:, aoff + 512 + kc * 256 + oc * 128: aoff + 512 + kc * 256 + oc * 128 + 128] for kc in range(2)], rhs, R=Ru)
                    op("act", lambda h, t0=t0, tn=tn, pr=prs[ti], f=f: h.activation(out=Rs[:, t0:t0 + tn], in_=PS[pr][:, 0:tn], func=AF.Tanh,
                                                                                  scale=0.5, bias=HBV[:, j, 0, f:f + 1]),
                       R=[bPS[prs[ti]], bVEC], W=[S["bR"][ti]])
                    op("act", lambda h, t0=t0, tn=tn, pi=pis[ti], f=f: h.activation(out=Is[:, t0:t0 + tn], in_=PS[pi][:, 0:tn], func=AF.Tanh,
                                                                                  scale=0.5, bias=HBV[:, j, 1, f:f + 1]),
                       R=[bPS[pis[ti]], bVEC], W=[S["bI"][ti]])
                yield
                bRa, bIa, bMa = list(S["bR"]), list(S["bI"]), list(S["bM"])
                op("act", lambda h, f=f: h.activation(out=Rs[:, 0:TP], in_=Rs[:, 0:TP], func=AF.Exp,
                                                      scale=NLC[:, j, f:f + 1], bias=NLC[:, j, f:f + 1]),
                   R=bRa + [bNLC], W=bRa)
                op("act", lambda h: h.activation(out=Ms[:, 0:TP], in_=Rs[:, 0:TP], func=AF.Square), R=bRa, W=bMa)
                yield
                op("act", lambda h: h.activation(out=Ms[:, 0:TP], in_=Ms[:, 0:TP], func=AF.Sqrt, scale=-0.25, bias=0.25), R=bMa, W=bMa)
                yield
                op("dve", lambda h: h.scalar_tensor_tensor(out=Is[:, 0:TP], in0=Is[:, 0:TP], scalar=1.0, in1=Ms[:, 0:TP],
                                                           op0=ALU.add, op1=ALU.mult), R=bIa + bMa, W=bIa)
                op("dve", lambda h, oc=oc: h.tensor_tensor(out=Is[:, 0:TP], in0=Is[:, 0:TP], in1=Us[oc][:, 0:TP], op=ALU.mult),
                   R=bIa + list(S["bU"][oc]), W=bIa)
                yield
                op("dve", lambda h: h.tensor_tensor_scan(out=Ms[:, 0:TP], data0=Rs[:, 0:TP], data1=Is[:, 0:TP],
                                                         initial=0.0, op0=ALU.mult, op1=ALU.add),
                   R=S["bR"] + S["bI"], W=S["bM"])
                op("dve", lambda h, f=f: h.tensor_copy(out=EXS[:, f, 3:4], in_=Ms[:, TP - 1:TP]), R=[S["bM"][2]], W=[bEXS])
                yield

        def rg_pass1(l):
            j = l // 2
            for q in range(2):
                s_ai = state["slot"]
                state["slot"] = (s_ai + 1) % NSLOT
                dma("pool", ring_sem[s_ai], RING[s_ai][:, 0:4096].rearrange("p (b e) -> p b e", e=1024),
                    wd("w_rgai")[j, 4 * q:4 * q + 4].rearrange("b p e -> p b e"), W=[bRING[s_ai]])
                for pair in range(2):
                    n0 = 4 * q + 2 * pair
                    gens = [rg_block1(l, n0, SETS[0], s_ai, (n0 % 4) * 1024),
                            rg_block1(l, n0 + 1, SETS[1], s_ai, ((n0 + 1) % 4) * 1024)]
                    alive = [True, True]
                    while any(alive):
                        for gi in range(2):
                            if alive[gi]:
                                try:
                                    next(gens[gi])
                                except StopIteration:
                                    alive[gi] = False

        def rg_mixer(l):
            j = l // 2
            dma("sp", msem(), STR[:].rearrange("p a b c -> p (a b c)"), st_rgc[j], W=[bSTR])
            dma("sp", msem(), STH[:].rearrange("p a b -> p (a b)"), st_rgh[j], W=[bSTH])
            rg_pass1(l)
            pre = (load_slab(wd("w_rgx")[j, 0], 4096), load_slab(wd("w_rgai")[j, 0], 1024), load_slab(wd("w_rgg")[j, 0], 4096))
            exchange(l)
            rg_pass(l, True, pre=pre)
            so1, so2 = P.semsrc("or%d" % j), P.semsrc("oh%d" % j)
            out_sems.extend([so1, so2])
            dma("sp", so1, o_rgc[j], OSR[:].rearrange("p a b c -> p (a b c)"), R=[bOSR])
            dma("sp", so2, o_rgh[j], OSH[:].rearrange("p a b -> p (a b)"), R=[bOSH])
            out_proj(wd("w_rgout")[j], NCH, bZ)

        def ffn(l):
            for q in range(NQ):
                for jj in range(QCH):
                    jx = q * QCH + jj
                    s = load_slab(wd("w_gu")[l, jx], 4096)
                    for ti, (t0, tn) in enumerate(TILES):
                        pg, pu = bank(), bank()
                        hn = [HN[:, kc, t0:t0 + tn] for kc in range(NCH)]
                        Rh = [bRING[s]] + [bHN[kc][ti] for kc in range(NCH)]
                        mm_group(pg, tn, [RING[s][:, kc * 256: kc * 256 + 128] for kc in range(NCH)], hn, R=Rh)
                        mm_group(pu, tn, [RING[s][:, kc * 256 + 128: kc * 256 + 256] for kc in range(NCH)], hn, R=Rh)
                        gi = state["g"] % 2
                        state["g"] += 1
                        g_ = Gt[:, gi, 0:tn]
                        op("act", lambda h, g_=g_, pg=pg, tn=tn: h.activation(out=g_, in_=PS[pg][:, 0:tn], func=AF.Silu),
                           R=[bPS[pg]], W=[bG[gi]])
                        op("dve", lambda h, g_=g_, jj=jj, t0=t0, tn=tn, pu=pu: h.tensor_tensor(
                            out=Z[:, jj, t0:t0 + tn], in0=PS[pu][:, 0:tn], in1=g_, op=ALU.mult),
                            R=[bPS[pu], bG[gi]], W=[bZ[jj][ti]])
                out_proj(wd("w_dn")[l, q], QCH, bZ)

        def ple(l):
            sp_ = msem("pool")
            dma("pool", sp_, UB[:].rearrange("p a b -> p (a b)").rearrange("p (a b) -> p a b", b=T),
                pT[l].rearrange("p (a b) -> p a b", b=T), W=[b for row in bUB for b in row])
            for n in range(8):
                s = load_slab(wd("w_pg")[l, n], 4096)
                s2 = load_slab(wd("w_pp")[l, n], 512)
                for cc in range(2):
                    ch = 2 * n + cc
                    for ti, (t0, tn) in enumerate(TILES):
                        pg, pp_ = bank(), bank()
                        mm_group(pg, tn, [RING[s][:, kc * 256 + cc * 128: kc * 256 + cc * 128 + 128] for kc in range(NCH)],
                                 [HN[:, kc, t0:t0 + tn] for kc in range(NCH)],
                                 R=[bRING[s]] + [bHN[kc][ti] for kc in range(NCH)])
                        mm_group(pp_, tn, [RING[s2][:, kc * 256 + cc * 128: kc * 256 + cc * 128 + 128] for kc in range(2)],
                                 [UB[:, kc, t0:t0 + tn] for kc in range(2)],
                                 R=[bRING[s2], bUB[0][ti], bUB[1][ti]])
                        gi = state["g"] % 2
                        state["g"] += 1
                        g_ = Gt[:, gi, 0:tn]
                        op("act", lambda h, g_=g_, pg=pg, tn=tn: h.activation(out=g_, in_=PS[pg][:, 0:tn], func=AF.Sigmoid),
                           R=[bPS[pg]], W=[bG[gi]])
                        op("dve", lambda h, g_=g_, pp_=pp_, tn=tn: h.tensor_tensor(out=g_, in0=PS[pp_][:, 0:tn], in1=g_, op=ALU.mult),
                           R=[bPS[pp_], bG[gi]], W=[bG[gi]])
                        op("dve", lambda h, g_=g_, ch=ch, t0=t0, tn=tn: h.tensor_tensor(
                            out=X[:, ch, t0:t0 + tn], in0=X[:, ch, t0:t0 + tn], in1=g_, op=ALU.add),
                            R=[bG[gi]], W=[bX[ch][ti]])

        phases = []
        for l in range(4):
            phases.append(lambda l=l: norm(V_MIX + l))
            phases.append((lambda l=l: conv_mixer(l)) if l % 2 == 0 else (lambda l=l: rg_mixer(l)))
            phases.append(lambda l=l: norm(V_FFN + l))
            phases.append(lambda l=l: ffn(l))
            phases.append(lambda l=l: norm(V_PLE + l))
            phases.append(lambda l=l: ple(l))
        phases.append(lambda: norm(V_FIN, dst="X"))
        for ph in (phases if stop is None else phases[:stop]):
            ph()
        y3 = y_d.rearrange("p (a b) -> p a b", b=T)
        for ti, (t0, tn) in enumerate(TILES):
            sy = P.semsrc("yout%d" % ti)
            out_sems.append(sy)
            dma("sp", sy, y3[:, :, t0:t0 + tn], X[:, :, t0:t0 + tn], R=[bX[c][ti] for c in range(NCH)])
        P.wait_all("sp", out_sems)

        with nc.Block() as block:
            @block.tensor
            def _(h):
                for f_ in P.E["pe"].prog:
                    f_(h)

            @block.scalar
            def _(h):
                for f_ in P.E["act"].prog:
                    f_(h)

            @block.vector
            def _(h):
                for f_ in P.E["dve"].prog:
                    f_(h)

            @block.gpsimd
            def _(h):
                for f_ in P.E["pool"].prog:
                    f_(h)

            @block.sync
            def _(h):
                for f_ in P.E["sp"].prog:
                    f_(h)
    return nc


def _slab(W, kc):
    n = W.shape[1]
    return W.reshape(kc, 128, n).transpose(1, 0, 2).reshape(128, kc * n)


def _feat_major(a):
    t, d = a.shape
    return a.T.reshape(d // 128, 128, t).transpose(1, 0, 2)


def _cols256(W):
    K = W.shape[0]
    kc = K // 128
    return np.ascontiguousarray(W.reshape(kc, 128, 8, 256).transpose(2, 1, 0, 3)).reshape(8, 128, kc * 256)


def prepare(x_prompt, x_sample, p_prompt, p_sample, state_conv, state_rg_conv, state_rg_h,
            mix_norm, ffn_norm, ple_norm, final_norm,
            sc_w_in, sc_w_conv, sc_w_out,
            rg_w_x, rg_w_gate, rg_conv_w, rg_conv_b, rg_w_a, rg_b_a, rg_w_i, rg_b_i, rg_lambda, rg_w_out,
            ffn_w_gate, ffn_w_up, ffn_w_down, ple_w_gate, ple_w_proj):
    f32 = np.float32
    A = lambda a: np.asarray(a, dtype=f32)
    x_prompt, x_sample, p_prompt, p_sample = A(x_prompt), A(x_sample), A(p_prompt), A(p_sample)
    state_conv, state_rg_conv, state_rg_h = A(state_conv), A(state_rg_conv), A(state_rg_h)

    vl = [A(mix_norm)[l] for l in range(4)] + [A(ffn_norm)[l] for l in range(4)] + [A(ple_norm)[l] for l in range(4)]
    vl += [A(final_norm)]
    vl += [A(sc_w_conv)[j, k] for j in range(2) for k in range(3)]
    vl += [A(rg_conv_w)[j, k] for j in range(2) for k in range(4)]
    vl += [A(rg_conv_b)[j] for j in range(2)] + [A(rg_b_a)[j] for j in range(2)] + [A(rg_b_i)[j] for j in range(2)]
    vl += [A(rg_lambda)[j] for j in range(2)]
    assert len(vl) == NV
    vecs = np.ascontiguousarray(np.stack(vl, 0).reshape(NV, NCH, 128).transpose(2, 0, 1)).reshape(128, NV * NCH)

    w_in = A(sc_w_in)
    Wc = w_in[:, :, 2048:4096].reshape(2, 16, 128, 16, 128)
    Wx = w_in[:, :, 4096:6144].reshape(2, 16, 128, 16, 128)
    w_sccx = np.ascontiguousarray(np.stack([Wc, Wx], axis=4).transpose(0, 3, 2, 1, 4, 5)).reshape(2, 16, 128, 4096)
    w_scb = np.stack([_cols256(w_in[j, :, 0:2048]) for j in range(2)])
    w_scout = np.stack([_cols256(A(sc_w_out)[j]) for j in range(2)])
    w_rgx = np.stack([_cols256(A(rg_w_x)[j]) for j in range(2)])
    w_rgg = np.stack([_cols256(A(rg_w_gate)[j]) for j in range(2)])
    w_rgout = np.stack([_cols256(A(rg_w_out)[j]) for j in range(2)])
    wa = A(rg_w_a).reshape(2, 8, 2, 128, 256).transpose(0, 1, 3, 2, 4).reshape(2, 8, 128, 512)
    wi = A(rg_w_i).reshape(2, 8, 2, 128, 256).transpose(0, 1, 3, 2, 4).reshape(2, 8, 128, 512)
    w_rgai = np.ascontiguousarray(np.concatenate([wa, wi], axis=3))
    Wg = A(ffn_w_gate).reshape(4, 16, 128, 44, 128)
    Wu = A(ffn_w_up).reshape(4, 16, 128, 44, 128)
    w_gu = np.ascontiguousarray(np.stack([Wg, Wu], axis=4).transpose(0, 3, 2, 1, 4, 5)).reshape(4, 44, 128, 4096)
    w_dn = np.ascontiguousarray(A(ffn_w_down).reshape(4, NQ, QCH, 128, 8, 256).transpose(0, 1, 4, 3, 2, 5)).reshape(
        4, NQ, 8, 128, QCH * 256)
    w_pg = np.stack([_cols256(A(ple_w_gate)[l]) for l in range(4)])
    w_pp = np.stack([_cols256(A(ple_w_proj)[l]) for l in range(4)])

    shared = dict(vecs=vecs, w_sccx=w_sccx, w_scb=w_scb, w_scout=w_scout, w_rgx=w_rgx, w_rgg=w_rgg,
                  w_rgai=w_rgai, w_rgout=w_rgout, w_gu=w_gu, w_dn=w_dn, w_pg=w_pg, w_pp=w_pp)

    in_maps = []
    for c in range(8):
        s, hf = c // 2, c % 2
        xt = np.concatenate([x_prompt[s, hf * TP:(hf + 1) * TP], x_sample[c * NS:(c + 1) * NS, 0]], axis=0)
        xT = np.ascontiguousarray(_feat_major(xt)).reshape(128, NCH * T)
        pt = np.concatenate([p_prompt[:, s, hf * TP:(hf + 1) * TP], p_sample[:, c * NS:(c + 1) * NS, 0]], axis=1)
        pTt = np.ascontiguousarray(pt.transpose(0, 2, 1).reshape(4, 2, 128, T).transpose(0, 2, 1, 3)).reshape(4, 128, 2 * T)
        sc = state_conv[:, c * NS:(c + 1) * NS]
        stc = np.ascontiguousarray(sc.reshape(2, NS, 2, NCH, 128).transpose(0, 4, 3, 2, 1)).reshape(2, 128, NCH * 2 * NS)
        sr = state_rg_conv[:, c * NS:(c + 1) * NS]
        strc = np.ascontiguousarray(sr.reshape(2, NS, 3, NCH, 128).transpose(0, 4, 3, 2, 1)).reshape(2, 128, NCH * 3 * NS)
        sh = state_rg_h[:, c * NS:(c + 1) * NS]
        sth = np.ascontiguousarray(sh.reshape(2, NS, NCH, 128).transpose(0, 3, 2, 1)).reshape(2, 128, NCH * NS)
        m = dict(shared)
        m.update(xT=xT, pT=pTt, st_conv=stc, st_rgc=strc, st_rgh=sth,
                 flag=np.full((128, 1), float(hf), f32))
        in_maps.append(m)
    return in_maps


def kernel(**inputs):
    in_maps = prepare(**inputs)
    nc = build_program()
    res = run_bass_kernel_spmd(nc, in_maps, core_ids=list(range(8)))
    return assemble(res.results)


def assemble(R, ncores=8):
    f32 = np.float32

    y_prompt = np.empty((4, 2048, D), f32)
    y_sample = np.empty((128, 1, D), f32)
    conv_p = np.empty((2, 4, 2, D), f32)
    conv_s = np.empty((2, 128, 2, D), f32)
    rgc_p = np.empty((2, 4, 3, D), f32)
    rgc_s = np.empty((2, 128, 3, D), f32)
    rgh_p = np.empty((2, 4, D), f32)
    rgh_s = np.empty((2, 128, D), f32)
    for c in range(ncores):
        s, hf = c // 2, c % 2
        y = np.asarray(R[c]["y"]).reshape(128, NCH, T).transpose(2, 1, 0).reshape(T, D)
        y_prompt[s, hf * TP:(hf + 1) * TP] = y[:TP]
        y_sample[c * NS:(c + 1) * NS, 0] = y[TP:]
        oc = np.asarray(R[c]["o_conv"]).reshape(2, 128, NCH, 2, 17).transpose(0, 4, 3, 2, 1).reshape(2, 17, 2, D)
        orc = np.asarray(R[c]["o_rgc"]).reshape(2, 128, NCH, 3, 17).transpose(0, 4, 3, 2, 1).reshape(2, 17, 3, D)
        oh = np.asarray(R[c]["o_rgh"]).reshape(2, 128, NCH, 17).transpose(0, 3, 2, 1).reshape(2, 17, D)
        conv_s[:, c * NS:(c + 1) * NS] = oc[:, 1:]
        rgc_s[:, c * NS:(c + 1) * NS] = orc[:, 1:]
        rgh_s[:, c * NS:(c + 1) * NS] = oh[:, 1:]
        if hf == 1:
            conv_p[:, s] = oc[:, 0]
            rgc_p[:, s] = orc[:, 0]
            rgh_p[:, s] = oh[:, 0]
    return (y_prompt, y_sample, conv_p, conv_s, rgc_p, rgc_s, rgh_p, rgh_s)
```

```python
import numpy as np
from contextlib import ExitStack
import concourse.bass as bass
import concourse.mybir as mybir
from concourse.bass_utils import run_bass_kernel_spmd

F32 = mybir.dt.float32
BF16 = mybir.dt.bfloat16
AF = mybir.ActivationFunctionType
COPY = AF.Identity
ALU = mybir.AluOpType

D = 2048
NCH = 16
TP = 1024
NS = 16
T = TP + NS
TILES = [(0, 347), (347, 347), (694, 346)]
PTILES = [(0, 347), (347, 347), (694, 330)]
DFF = 5632
NQ = 4
QCH = 11
EPS = 1e-6
NV = 35
SLOT = 4096
NSLOT = 3
DECLARED = []
DBG = set()

V_MIX, V_FFN, V_PLE, V_FIN, V_SCW, V_RGW, V_RGB, V_BA, V_BI, V_LAM = 0, 4, 8, 12, 13, 19, 27, 29, 31, 33


class Eng:
    def __init__(self, name, sem):
        self.name = name
        self.sem = sem
        self.cnt = 0
        self.seen = {}
        self.prog = []


class SemSrc:
    def __init__(self, name, sem):
        self.name = name
        self.sem = sem
        self.cnt = 0


class Buf:
    __slots__ = ("w", "r", "const", "consumed")

    def __init__(self, const=False):
        self.w = None
        self.r = []
        self.const = const
        self.consumed = True


class Prog:
    def __init__(self, nc, st):
        self.nc = nc
        self.st = st
        self.E = {}
        for n in ("pe", "act", "dve", "pool", "sp"):
            self.E[n] = Eng(n, st.enter_context(nc.semaphore("sem_" + n)))
        self.nsem = 0

    def semsrc(self, name):
        self.nsem += 1
        return SemSrc(name, self.st.enter_context(self.nc.semaphore("ds_%s_%d" % (name, self.nsem))))

    def _deps(self, E, R, W):
        deps = {}

        def add(d):
            if d is None:
                return
            src, tk = d
            if deps.get(src, 0) < tk:
                deps[src] = tk
        for b in R:
            add(b.w)
        for b in W:
            add(b.w)
            for d in b.r:
                add(d)
        waits = []
        for src, tk in deps.items():
            if src is E and E.name == "pe":
                continue
            if E.seen.get(src, 0) >= tk:
                continue
            E.seen[src] = tk
            waits.append((src.sem, tk))
        return waits

    def op(self, en, fn, R=(), W=(), tag=None):
        if tag is not None and ("skip:" + tag) in DBG:
            return
        E = self.E[en]
        waits = self._deps(E, R, W)
        E.cnt += 1
        my = (E, E.cnt)
        sem = E.sem

        def run(h, waits=waits, fn=fn, sem=sem):
            for s_, v in waits:
                h.wait_ge(s_, v)
            fn(h).then_inc(sem, 1)
        E.prog.append(run)
        for b in R:
            b.consumed = True
            if not b.const:
                b.r.append(my)
        for b in W:
            b.w = my
            b.r = []
            b.consumed = False

    def dma(self, qn, S, out, in_, R=(), W=()):
        Q = self.E[qn]
        waits = self._deps(Q, R, W)
        if S.cnt > 0 and Q.seen.get(S, 0) < S.cnt:
            Q.seen[S] = S.cnt
            waits.append((S.sem, S.cnt))
        S.cnt += 16
        my = (S, S.cnt)
        sem = S.sem

        def run(h, waits=waits, sem=sem, out=out, in_=in_):
            for s_, v in waits:
                h.wait_ge(s_, v)
            h.dma_start(out=out, in_=in_).then_inc(sem, 16)
        Q.prog.append(run)
        for b in R:
            if not b.const:
                b.r.append(my)
        for b in W:
            b.w = my
            b.r = []

    def wait_all(self, en, srcs):
        E = self.E[en]
        waits = [(s.sem, s.cnt) for s in srcs if s.cnt > 0]

        def run(h, waits=waits):
            for s_, v in waits:
                h.wait_ge(s_, v)
        E.prog.append(run)


def _split_cols(n):
    for b in (2048, 1408, 1040, 1024, 512):
        if n % b == 0 and b <= n:
            return b
    raise ValueError(n)


def build_program(n_cores=8, stop=None):
    del DECLARED[:]
    nc = bass.Bass("TRN2", target_bir_lowering=False)

    def din(name, shape):
        return nc.dram_tensor(name, list(shape), F32, kind="ExternalInput").ap()

    def dout(name, shape):
        return nc.dram_tensor(name, list(shape), F32, kind="ExternalOutput").ap()

    xT = din("xT", [128, NCH * T])
    pT = din("pT", [4, 128, 2 * T])
    vecs = din("vecs", [128, NV * NCH])
    st_conv = din("st_conv", [2, 128, NCH * 2 * NS])
    st_rgc = din("st_rgc", [2, 128, NCH * 3 * NS])
    st_rgh = din("st_rgh", [2, 128, NCH * NS])
    flag_d = din("flag", [128, 1])
    WSHAPES = {
        "w_sccx": [2, 16, 128, 4096],
        "w_scb": [2, 8, 128, 4096],
        "w_scout": [2, 8, 128, 4096],
        "w_rgx": [2, 8, 128, 4096],
        "w_rgg": [2, 8, 128, 4096],
        "w_rgai": [2, 8, 128, 1024],
        "w_rgout": [2, 8, 128, 4096],
        "w_gu": [4, 44, 128, 4096],
        "w_dn": [4, NQ, 8, 128, QCH * 256],
        "w_pg": [4, 8, 128, 4096],
        "w_pp": [4, 8, 128, 512],
    }
    wcache = {}

    def wd(name):
        if name not in wcache:
            wcache[name] = din(name, WSHAPES[name])
            DECLARED.append(name)
        return wcache[name]

    y_d = dout("y", [128, NCH * T])
    o_conv = dout("o_conv", [2, 128, NCH * 2 * 17])
    o_rgc = dout("o_rgc", [2, 128, NCH * 3 * 17])
    o_rgh = dout("o_rgh", [2, 128, NCH * 17])

    ccsrc = [nc.dram_tensor("ccsrc%d" % l, [128, NCH * 4], F32, kind="Internal").ap() for l in range(4)]
    ccdst = [nc.dram_tensor("ccdst%d" % l, [256, NCH * 4], F32, kind="Internal", addr_space="Local").ap()
             for l in range(4)]

    with ExitStack() as st:
        def sb(name, shape, dt=F32):
            return st.enter_context(nc.sbuf_tensor(name, list(shape), dt))

        X = sb("X", [128, NCH, T])
        HN = sb("HN", [128, NCH, T], BF16)
        Z = sb("Z", [128, NCH, T], BF16)
        RING = [sb("RING%d" % i, [128, SLOT], BF16) for i in range(NSLOT)]
        VEC = sb("VEC", [128, NV, NCH])
        NLC = sb("NLC", [128, 2, NCH])
        HBV = sb("HBV", [128, 2, 2, NCH])
        STC = sb("STC", [128, NCH, 2, NS])
        STR = sb("STR", [128, NCH, 3, NS])
        STH = sb("STH", [128, NCH, NS])
        OSC = sb("OSC", [128, NCH, 2, 17])
        OSR = sb("OSR", [128, NCH, 3, 17])
        OSH = sb("OSH", [128, NCH, 17])
        ONES = sb("ONES", [128, 128], BF16)
        EXS = sb("EXS", [128, NCH, 4])
        EXR = sb("EXR", [128, NCH, 4])
        HALO = sb("HALO", [128, NCH, 4])
        YS = sb("YS", [128, NCH, 2])
        BS = sb("BS", [128, NCH, 2])
        FT = sb("FT", [128, 4, NCH])
        FLAG = sb("FLAG", [128, 1])
        UX = sb("UX", [128, 3 + T])
        U = [sb("U%d" % i, [128, T]) for i in range(2)]
        UB = sb("UB", [128, 2, T], BF16)
        Rb = sb("Rb", [128, T])
        Ib = sb("Ib", [128, T])
        Mb = sb("Mb", [128, T])
        Gt = sb("Gt", [128, 2, 512])
        PS = [st.enter_context(nc.psum_tensor("PS%d" % i, [128, 512], F32)) for i in range(8)]

        P = Prog(nc, st)
        op, dma = P.op, P.dma

        bX = [[Buf() for _ in TILES] for _ in range(NCH)]
        bHN = [[Buf() for _ in TILES] for _ in range(NCH)]
        bZ = [[Buf() for _ in TILES] for _ in range(NCH)]
        bRING = [Buf() for _ in range(NSLOT)]
        bVEC, bNLC, bONES, bFLAG = Buf(True), Buf(True), Buf(True), Buf(True)
        bSTC, bSTR, bSTH = Buf(), Buf(), Buf()
        bOSC, bOSR, bOSH = Buf(), Buf(), Buf()
        bEXS, bEXR, bHALO, bYS, bBS, bFT = Buf(), Buf(), Buf(), Buf(), Buf(), Buf()
        bUX = Buf()
        bU = [[Buf() for _ in TILES] for _ in range(2)]
        bUB = [[Buf() for _ in TILES] for _ in range(2)]
        bR = [Buf() for _ in TILES]
        bI = [Buf() for _ in TILES]
        bM = [Buf() for _ in TILES]
        bG = [Buf(), Buf()]
        bPS = [Buf() for _ in range(8)]
        bCS = [Buf() for _ in range(4)]
        bCD = [Buf() for _ in range(4)]
        bY = Buf()

        ring_sem = [P.semsrc("ring%d" % i) for i in range(NSLOT)]
        misc_sems = {"sp": [P.semsrc("msp%d" % i) for i in range(4)],
                     "pool": [P.semsrc("mpl%d" % i) for i in range(4)]}
        out_sems = []
        state = {"bank": 0, "slot": 0, "misc": 0, "g": 0}

        def bank():
            b = state["bank"]
            state["bank"] = (b + 1) % 8
            assert bPS[b].consumed, "PSUM bank %d re-allocated before its reader was emitted" % b
            return b

        def msem(q="sp"):
            s = misc_sems[q][state["misc"] % 4]
            state["misc"] += 1
            return s

        def load_slab(src2d, nelem, avoid=None):
            s = state["slot"]
            if s == avoid:
                s = (s + 1) % NSLOT
            state["slot"] = (s + 1) % NSLOT
            b = _split_cols(nelem)
            dma("pool", ring_sem[s],
                RING[s][:, 0:nelem].rearrange("p (a b) -> p a b", b=b),
                src2d.rearrange("p (a b) -> p a b", b=b), W=[bRING[s]])
            return s

        def mm_group(pbank, ncols, lhs_list, rhs_list, R, col0=0):
            n = len(lhs_list)

            def fn(h):
                ins = None
                for k in range(n):
                    ins = h.matmul(PS[pbank][:, col0:col0 + ncols], lhsT=lhs_list[k], rhs=rhs_list[k],
                                   start=(k == 0), stop=(k == n - 1))
                return ins
            op("pe", fn, R=R, W=[bPS[pbank]])

        def vcol(idx, c):
            return VEC[:, idx, c:c + 1]

        dma("sp", msem(), VEC[:].rearrange("p a b -> p (a b)"), vecs, W=[bVEC])
        dma("sp", msem(), FLAG[:], flag_d, W=[bFLAG])
        sx = P.semsrc("xload")
        for c in range(NCH):
            pass
        xT3 = xT.rearrange("p (a b) -> p a b", b=T)
        for ti, (t0, tn) in enumerate(TILES):
            sx = P.semsrc("xload%d" % ti)
            dma("sp", sx, X[:, :, t0:t0 + tn], xT3[:, :, t0:t0 + tn], W=[bX[c][ti] for c in range(NCH)])
        op("dve", lambda h: h.memset(ONES[:], 1.0), W=[bONES])
        op("dve", lambda h: h.memset(EXS[:].rearrange("p a b -> p (a b)"), 0.0), W=[bEXS])
        for j in range(2):
            op("act", lambda h, j=j: h.activation(out=NLC[:, j, :], in_=VEC[:, V_LAM + j, :], func=AF.Exp, scale=-1.0),
               R=[bVEC], W=[bNLC])
            op("act", lambda h, j=j: h.activation(out=NLC[:, j, :], in_=NLC[:, j, :], func=AF.Ln, bias=1.0),
               R=[bNLC], W=[bNLC])
            op("dve", lambda h, j=j: h.tensor_scalar(out=NLC[:, j, :], in0=NLC[:, j, :], scalar1=-4.0, scalar2=None,
                                                     op0=ALU.mult), R=[bNLC], W=[bNLC])

        for j in range(2):
            for q_, vi in ((0, V_BA), (1, V_BI)):
                op("dve", lambda h, j=j, q_=q_, vi=vi: h.tensor_scalar(out=HBV[:, j, q_, :], in0=VEC[:, vi + j, :], scalar1=0.5,
                                                                       scalar2=None, op0=ALU.mult), R=[bVEC], W=[bVEC])
        def norm(gidx, dst="HN"):
            for ti, (t0, tn) in enumerate(TILES):
                pb = bank()
                for c in range(NCH):
                    q = c % 4
                    h_, tq = q // 2, q % 2
                    sqap = UB[:, h_, TILES[tq][0]: TILES[tq][0] + tn]
                    op("act", lambda h, c=c, sqap=sqap, t0=t0, tn=tn: h.activation(
                        out=sqap, in_=X[:, c, t0:t0 + tn], func=AF.Square),
                        R=[bX[c][ti]], W=[bUB[h_][tq]])

                    def fn(h, c=c, sqap=sqap, pb=pb, tn=tn):
                        return h.matmul(PS[pb][:, 0:tn], lhsT=ONES[:], rhs=sqap, start=(c == 0), stop=(c == NCH - 1))
                    op("pe", fn, R=[bUB[h_][tq], bONES], W=[bPS[pb]] if c == 0 else [bPS[pb]])
                rs = Gt[:, 0, 0:tn]
                op("act", lambda h, pb=pb, tn=tn, rs=rs: h.activation(out=rs, in_=PS[pb][:, 0:tn], func=AF.Sqrt,
                                                                       scale=1.0 / D, bias=EPS),
                   R=[bPS[pb]], W=[bG[0]])
                op("dve", lambda h, rs=rs: h.reciprocal(out=rs, in_=rs), R=[bG[0]], W=[bG[0]])
                for c in range(NCH):
                    if dst == "HN":
                        o = HN[:, c, t0:t0 + tn]
                        Wb = [bHN[c][ti]]
                    else:
                        o = X[:, c, t0:t0 + tn]
                        Wb = [bX[c][ti]]
                    op("dve", lambda h, c=c, o=o, t0=t0, tn=tn, rs=rs: h.scalar_tensor_tensor(
                        out=o, in0=X[:, c, t0:t0 + tn], scalar=vcol(gidx, c), in1=rs, op0=ALU.mult, op1=ALU.mult),
                        R=[bX[c][ti], bG[0], bVEC], W=Wb)

        def out_proj(wd, nk, zb, pre=(), defer_t0=0, late=0):
            if "nooutproj" in DBG:
                return

            def grp(s, n, cc, ti, kcs=None, pb=None, first=True, last=True):
                ch = 2 * n + cc
                t0, tn = TILES[ti]
                if pb is None:
                    pb = bank()
                kcs = list(range(nk)) if kcs is None else kcs
                nkk = len(kcs)

                def fn(h):
                    ins = None
                    for i_, kc in enumerate(kcs):
                        ins = h.matmul(PS[pb][:, 0:tn], lhsT=RING[s][:, kc * 256 + cc * 128: kc * 256 + cc * 128 + 128],
                                       rhs=Z[:, kc, t0:t0 + tn], start=(first and i_ == 0), stop=(last and i_ == nkk - 1))
                    return ins
                op("pe", fn, R=[bRING[s]] + [zb[kc][ti] for kc in kcs], W=[bPS[pb]])
                if last:
                    op("dve", lambda h, ch=ch, t0=t0, tn=tn, pb=pb: h.tensor_tensor(
                        out=X[:, ch, t0:t0 + tn], in0=PS[pb][:, 0:tn], in1=X[:, ch, t0:t0 + tn], op=ALU.add),
                        R=[bPS[pb]], W=[bX[ch][ti]])
                return pb
            assert defer_t0 <= len(pre)
            for ti_set in ((1, 2), (0,)):
                for n in range(defer_t0):
                    for cc in range(2):
                        for ti in ti_set:
                            grp(pre[n], n, cc, ti)
            for n in range(defer_t0, 8):
                s = pre[n] if n < len(pre) else load_slab(wd[n], nk * 256)
                if late and n == defer_t0:
                    early, lat = list(range(nk - late)), list(range(nk - late, nk))
                    pbs = {}
                    for cc in range(2):
                        for ti in (0, 1, 2):
                            pbs[(cc, ti)] = grp(s, n, cc, ti, kcs=early, first=True, last=False)
                    for cc in range(2):
                        for ti in (0, 1, 2):
                            grp(s, n, cc, ti, kcs=lat, pb=pbs[(cc, ti)], first=False, last=True)
                    continue
                for cc in range(2):
                    for ti in (0, 1, 2):
                        grp(s, n, cc, ti)

        def exchange(l):
            if "noexch" in DBG:
                op("dve", lambda h: h.memset(HALO[:].rearrange("p a b -> p (a b)"), 0.0), W=[bHALO])
                return
            S1 = msem("pool")
            dma("pool", S1, ccsrc[l], EXS[:].rearrange("p a b -> p (a b)"), R=[bEXS], W=[bCS[l]])
            op("pool", lambda h, l=l: h.collective_compute(
                "AllGather", ALU.bypass, replica_groups=[[2 * i, 2 * i + 1] for i in range(n_cores // 2)],
                ins=[ccsrc[l]], outs=[ccdst[l]]), R=[bCS[l]], W=[bCD[l]])
            S2 = msem("pool")
            dma("pool", S2, EXR[:].rearrange("p a b -> p (a b)"), ccdst[l][0:128, :], R=[bCD[l]], W=[bEXR])
            op("dve", lambda h: h.tensor_scalar(out=HALO[:].rearrange("p a b -> p (a b)"),
                                                in0=EXR[:].rearrange("p a b -> p (a b)"),
                                                scalar1=FLAG[:, 0:1], scalar2=None, op0=ALU.mult),
               R=[bEXR, bFLAG], W=[bHALO])

        def conv_mixer(l):
            j = l // 2
            iw = V_SCW + 3 * j
            dma("sp", msem(), STC[:].rearrange("p a b c -> p (a b c)"), st_conv[j], W=[bSTC])
            CX = UX
            op("dve", lambda h: h.memset(CX[:, 0:2], 0.0), W=[bUX])
            sb_ = None
            for f in range(NCH if "nf" not in DBG else 2):
                s_cx = load_slab(wd("w_sccx")[j, f], 4096)
                if f % 2 == 0:
                    sb_ = load_slab(wd("w_scb")[j, f // 2], 4096)
                cc = f % 2
                pcs, pxs, pbks = {}, {}, {}

                def pe_b(ti):
                    t0p, tnp = TILES[ti]
                    pbks[ti] = bank()
                    mm_group(pbks[ti], tnp, [RING[sb_][:, kc * 256 + cc * 128: kc * 256 + cc * 128 + 128]
                                             for kc in range(NCH)], [HN[:, kc, t0p:t0p + tnp] for kc in range(NCH)],
                             R=[bRING[sb_]] + [bHN[kc][ti] for kc in range(NCH)])

                def ew_A(ti):
                    t0, tn = TILES[ti]
                    pc, px = pcs[ti], pxs[ti]
                    csb = U[0][:, t0:t0 + tn]
                    tmp = U[1][:, t0:t0 + tn]
                    op("act", lambda h, csb=csb, pc=pc, tn=tn: h.activation(out=csb, in_=PS[pc][:, 0:tn], func=COPY),
                       R=[bPS[pc]], W=[bU[0][ti]], tag="evac")
                    cxo = CX[:, 2 + t0: 2 + t0 + tn]
                    op("dve", lambda h, cxo=cxo, csb=csb, px=px, tn=tn: h.tensor_tensor(
                        out=cxo, in0=PS[px][:, 0:tn], in1=csb, op=ALU.mult),
                        R=[bPS[px], bU[0][ti]], W=[bUX], tag="cx")
                    pn = min(tn, TP - t0)
                    has_s = t0 + tn > TP
                    ptmp = U[1][:, t0:t0 + pn]
                    op("dve", lambda h, ptmp=ptmp, t0=t0, pn=pn, f=f: h.tensor_scalar(
                        out=ptmp, in0=CX[:, t0:t0 + pn], scalar1=vcol(iw, f), scalar2=None, op0=ALU.mult),
                        R=[bUX, bVEC], W=[bU[1][ti]], tag="conv")
                    for k in (1, 2):
                        op("dve", lambda h, ptmp=ptmp, t0=t0, pn=pn, f=f, k=k: h.scalar_tensor_tensor(
                            out=ptmp, in0=CX[:, t0 + k:t0 + k + pn], scalar=vcol(iw + k, f), in1=ptmp,
                            op0=ALU.mult, op1=ALU.add), R=[bUX, bU[1][ti], bVEC], W=[bU[1][ti]], tag="conv")
                    if has_s:
                        stmp = U[1][:, TP:T]
                        cxs = CX[:, 2 + TP:2 + T]
                        op("dve", lambda h, stmp=stmp, f=f: h.tensor_scalar(
                            out=stmp, in0=STC[:, f, 0, :], scalar1=vcol(iw, f), scalar2=None, op0=ALU.mult),
                            R=[bSTC, bVEC], W=[bU[1][ti]], tag="conv")
                        op("dve", lambda h, stmp=stmp, f=f: h.scalar_tensor_tensor(
                            out=stmp, in0=STC[:, f, 1, :], scalar=vcol(iw + 1, f), in1=stmp, op0=ALU.mult, op1=ALU.add),
                            R=[bSTC, bU[1][ti], bVEC], W=[bU[1][ti]], tag="conv")
                        op("dve", lambda h, stmp=stmp, f=f, cxs=cxs: h.scalar_tensor_tensor(
                            out=stmp, in0=cxs, scalar=vcol(iw + 2, f), in1=stmp, op0=ALU.mult, op1=ALU.add),
                            R=[bUX, bU[1][ti], bVEC], W=[bU[1][ti]], tag="conv")
                        op("act", lambda h, f=f: h.activation(out=OSC[:, f, 0, 1:17], in_=STC[:, f, 1, :], func=COPY),
                           R=[bSTC], W=[bOSC], tag="small1")
                        op("act", lambda h, f=f, cxs=cxs: h.activation(out=OSC[:, f, 1, 1:17], in_=cxs, func=COPY),
                           R=[bUX], W=[bOSC], tag="small1")
                    if has_s:
                        op("dve", lambda h, f=f: h.tensor_copy(out=EXS[:, f, 0:2], in_=CX[:, TP:TP + 2]),
                           R=[bUX], W=[bEXS], tag="small2")
                        op("dve", lambda h, f=f: h.tensor_copy(out=OSC[:, f, :, 0], in_=CX[:, TP:TP + 2]),
                           R=[bUX], W=[bOSC], tag="small3")

                def ew_B(ti):
                    t0, tn = TILES[ti]
                    pbk = pbks[ti]
                    tmp = U[1][:, t0:t0 + tn]
                    if ti == 0:
                        op("dve", lambda h, f=f: h.tensor_copy(out=YS[:, f, :], in_=U[1][:, 0:2]),
                           R=[bU[1][ti]], W=[bYS], tag="small2")
                        op("dve", lambda h, f=f, pbk=pbk: h.tensor_copy(out=BS[:, f, :], in_=PS[pbk][:, 0:2]),
                           R=[bPS[pbk]], W=[bBS], tag="small2")
                    op("dve", lambda h, f=f, t0=t0, tn=tn, tmp=tmp, pbk=pbk: h.tensor_tensor(
                        out=Z[:, f, t0:t0 + tn], in0=PS[pbk][:, 0:tn], in1=tmp, op=ALU.mult),
                        R=[bPS[pbk], bU[1][ti]], W=[bZ[f][ti]], tag="z")


                for ti, (t0, tn) in enumerate(TILES):
                    pcs[ti], pxs[ti] = bank(), bank()
                    hn = [HN[:, kc, t0:t0 + tn] for kc in range(NCH)]
                    Rh = [bHN[kc][ti] for kc in range(NCH)]
                    mm_group(pcs[ti], tn, [RING[s_cx][:, kc * 256: kc * 256 + 128] for kc in range(NCH)], hn,
                             R=[bRING[s_cx]] + Rh)
                    mm_group(pxs[ti], tn, [RING[s_cx][:, kc * 256 + 128: kc * 256 + 256] for kc in range(NCH)], hn,
                             R=[bRING[s_cx]] + Rh)
                    ew_A(ti)
                    if ti > 0:
                        pe_b(ti - 1)
                        ew_B(ti - 1)
                pe_b(2)
                ew_B(2)
            so = P.semsrc("oc%d" % j)
            out_sems.append(so)
            dma("sp", so, o_conv[j], OSC[:].rearrange("p a b c -> p (a b c)"), R=[bOSC])
            pre = [load_slab(wd("w_scout")[j][n_], NCH * 256) for n_ in range(2)]
            exchange(l)
            w0 = VEC[:, iw, :]
            w1 = VEC[:, iw + 1, :]
            h0 = HALO[:, :, 0]
            h1 = HALO[:, :, 1]
            op("dve", lambda h: h.tensor_tensor(out=FT[:, 0, :], in0=w0, in1=h0, op=ALU.mult), R=[bHALO, bVEC], W=[bFT])
            op("dve", lambda h: h.tensor_tensor(out=FT[:, 1, :], in0=w1, in1=h1, op=ALU.mult), R=[bHALO, bVEC, bFT], W=[bFT])
            op("dve", lambda h: h.tensor_tensor(out=FT[:, 0, :], in0=FT[:, 0, :], in1=FT[:, 1, :], op=ALU.add), R=[bFT], W=[bFT])
            op("dve", lambda h: h.tensor_tensor(out=FT[:, 2, :], in0=w0, in1=h1, op=ALU.mult), R=[bHALO, bVEC, bFT], W=[bFT])
            op("dve", lambda h: h.tensor_tensor(out=YS[:, :, 0], in0=YS[:, :, 0], in1=FT[:, 0, :], op=ALU.add), R=[bFT, bYS], W=[bYS])
            op("dve", lambda h: h.tensor_tensor(out=YS[:, :, 1], in0=YS[:, :, 1], in1=FT[:, 2, :], op=ALU.add), R=[bFT, bYS], W=[bYS])
            op("dve", lambda h: h.tensor_tensor(out=Z[:, :, 0:2], in0=YS[:], in1=BS[:], op=ALU.mult),
               R=[bYS, bBS], W=[bZ[f][0] for f in range(NCH)])
            out_proj(wd("w_scout")[j], NCH, bZ, pre=pre, defer_t0=2)

        def rg_pass(l, final, pre=None):
            j = l // 2
            iw = V_RGW + 4 * j
            ib = V_RGB + j
            tiles = list(enumerate(TILES)) if final else list(enumerate(TILES))[:2]
            for n in range(8):
                if n == 0 and pre is not None:
                    s_x, s_ai, s_g = pre
                else:
                    s_x = load_slab(wd("w_rgx")[j, n], 4096)
                    s_ai = load_slab(wd("w_rgai")[j, n], 1024)
                    s_g = load_slab(wd("w_rgg")[j, n], 4096) if final else None
                for cc in range(2):
                    f = 2 * n + cc
                    if final:
                        op("dve", lambda h, f=f: h.tensor_copy(out=UX[:, 0:3], in_=HALO[:, f, 0:3]),
                           R=[bHALO], W=[bUX])
                    else:
                        op("dve", lambda h: h.memset(UX[:, 0:3], 0.0), W=[bUX])
                    pbs, pgs = {}, {}
                    hnr = lambda t0, tn: [HN[:, kc, t0:t0 + tn] for kc in range(NCH)]
                    for ti, (t0, tn) in tiles:
                        pbs[ti] = bank()
                        mm_group(pbs[ti], tn, [RING[s_x][:, kc * 256 + cc * 128: kc * 256 + cc * 128 + 128] for kc in range(NCH)],
                                 hnr(t0, tn), R=[bRING[s_x]] + [bHN[kc][ti] for kc in range(NCH)])
                    for ti, (t0, tn) in tiles:
                        op("act", lambda h, t0=t0, tn=tn, pb=pbs[ti]: h.activation(out=UX[:, 3 + t0:3 + t0 + tn], in_=PS[pb][:, 0:tn], func=COPY),
                           R=[bPS[pbs[ti]]], W=[bUX])
                    bUall = [bU[cc][0], bU[cc][1], bU[cc][2]]
                    op("act", lambda h, cc=cc, f=f: h.activation(out=U[cc][:, 0:TP], in_=UX[:, 0:TP], func=COPY,
                                                                  scale=vcol(iw, f), bias=vcol(ib, f)),
                       R=[bUX, bVEC], W=bUall)
                    for k in (1, 2, 3):
                        op("dve", lambda h, cc=cc, f=f, k=k: h.scalar_tensor_tensor(
                            out=U[cc][:, 0:TP], in0=UX[:, k:k + TP], scalar=vcol(iw + k, f), in1=U[cc][:, 0:TP],
                            op0=ALU.mult, op1=ALU.add), R=[bUX, bVEC] + bUall, W=bUall)
                    for ti, (t0, tn) in tiles:
                        if t0 + tn > TP:
                            uo = U[cc][:, TP:T]
                            uxo = UX[:, 3 + TP:3 + T]
                            op("dve", lambda h, uo=uo, f=f: h.tensor_scalar(
                                out=uo, in0=STR[:, f, 0, :], scalar1=vcol(iw, f), scalar2=vcol(ib, f),
                                op0=ALU.mult, op1=ALU.add), R=[bSTR, bVEC], W=[bU[cc][ti]])
                            for k in (1, 2):
                                op("dve", lambda h, uo=uo, f=f, k=k: h.scalar_tensor_tensor(
                                    out=uo, in0=STR[:, f, k, :], scalar=vcol(iw + k, f), in1=uo,
                                    op0=ALU.mult, op1=ALU.add), R=[bSTR, bU[cc][ti], bVEC], W=[bU[cc][ti]])
                            op("dve", lambda h, uo=uo, f=f, uxo=uxo: h.scalar_tensor_tensor(
                                out=uo, in0=uxo, scalar=vcol(iw + 3, f), in1=uo, op0=ALU.mult, op1=ALU.add),
                                R=[bUX, bU[cc][ti], bVEC], W=[bU[cc][ti]])
                            op("act", lambda h, f=f: h.activation(out=OSR[:, f, 0:2, 1:17], in_=STR[:, f, 1:3, :], func=COPY),
                               R=[bSTR], W=[bOSR])
                            op("act", lambda h, f=f, uxo=uxo: h.activation(out=OSR[:, f, 2, 1:17], in_=uxo, func=COPY),
                               R=[bUX], W=[bOSR])
                    op("act", lambda h, cc=cc: h.activation(out=UB[:, cc, 0:T], in_=U[cc][:, 0:T], func=COPY),
                       R=bUall, W=[bUB[cc][0], bUB[cc][1], bUB[cc][2]])
                    if final:
                        op("dve", lambda h, f=f: h.tensor_copy(out=OSR[:, f, :, 0], in_=UX[:, TP:TP + 3]), R=[bUX], W=[bOSR])
                    else:
                        op("dve", lambda h, f=f: h.tensor_copy(out=EXS[:, f, 0:3], in_=UX[:, TP:TP + 3]), R=[bUX], W=[bEXS])
                if final:
                    for cc in range(2):
                        f = 2 * n + cc
                        pgs = {}
                        for ti, (t0, tn) in tiles:
                            pgs[ti] = bank()
                            mm_group(pgs[ti], tn, [RING[s_g][:, kc * 256 + cc * 128: kc * 256 + cc * 128 + 128] for kc in range(NCH)],
                                     [HN[:, kc, t0:t0 + tn] for kc in range(NCH)], R=[bRING[s_g]] + [bHN[kc][ti] for kc in range(NCH)])
                        for ti, (t0, tn) in tiles:
                            op("act", lambda h, f=f, t0=t0, tn=tn, pg=pgs[ti]: h.activation(out=Z[:, f, t0:t0 + tn], in_=PS[pg][:, 0:tn], func=AF.Gelu_apprx_tanh),
                               R=[bPS[pgs[ti]], bHALO], W=[bZ[f][ti]])
                for oc in range(2):
                    f = 2 * n + oc
                    prs, pis = {}, {}
                    for ti, (t0, tn) in tiles:
                        prs[ti], pis[ti] = bank(), bank()
                        rhs = [UB[:, kc, t0:t0 + tn] for kc in range(2)]
                        Ru = [bRING[s_ai], bUB[0][ti], bUB[1][ti]]
                        mm_group(prs[ti], tn, [RING[s_ai][:, kc * 256 + oc * 128: kc * 256 + oc * 128 + 128] for kc in range(2)], rhs, R=Ru)
                        mm_group(pis[ti], tn, [RING[s_ai][:, 512 + kc * 256 + oc * 128: 512 + kc * 256 + oc * 128 + 128] for kc in range(2)], rhs, R=Ru)
                    sl = lambda B_, t0, tn: B_[:, t0:t0 + tn]
                    for ti, (t0, tn) in tiles:
                        op("act", lambda h, t0=t0, tn=tn, pr=prs[ti], f=f: h.activation(out=sl(Rb, t0, tn), in_=PS[pr][:, 0:tn], func=AF.Tanh,
                                                                                      scale=0.5, bias=HBV[:, j, 0, f:f + 1]),
                           R=[bPS[prs[ti]], bVEC], W=[bR[ti]])
                    for ti, (t0, tn) in tiles:
                        op("act", lambda h, t0=t0, tn=tn, pi=pis[ti], f=f: h.activation(out=sl(Ib, t0, tn), in_=PS[pi][:, 0:tn], func=AF.Tanh,
                                                                                      scale=0.5, bias=HBV[:, j, 1, f:f + 1]),
                           R=[bPS[pis[ti]], bVEC], W=[bI[ti]])
                    bRa, bIa, bMa = list(bR), list(bI), list(bM)
                    op("act", lambda h, f=f: h.activation(out=Rb[:, 0:T], in_=Rb[:, 0:T], func=AF.Exp,
                                                          scale=NLC[:, j, f:f + 1], bias=NLC[:, j, f:f + 1]),
                       R=bRa + [bNLC], W=bRa)
                    op("act", lambda h: h.activation(out=Mb[:, 0:T], in_=Rb[:, 0:T], func=AF.Square), R=bRa, W=bMa)
                    op("act", lambda h: h.activation(out=Mb[:, 0:T], in_=Mb[:, 0:T], func=AF.Sqrt, scale=-0.25, bias=0.25), R=bMa, W=bMa)
                    op("dve", lambda h: h.scalar_tensor_tensor(out=Ib[:, 0:T], in0=Ib[:, 0:T], scalar=1.0, in1=Mb[:, 0:T],
                                                               op0=ALU.add, op1=ALU.mult), R=bIa + bMa, W=bIa)
                    op("dve", lambda h, oc=oc: h.tensor_tensor(out=Ib[:, 0:T], in0=Ib[:, 0:T], in1=U[oc][:, 0:T], op=ALU.mult),
                       R=bIa + [bU[oc][0], bU[oc][1], bU[oc][2]], W=bIa)
                    init = HALO[:, f, 3:4] if final else 0.0
                    op("dve", lambda h, init=init: h.tensor_tensor_scan(out=Mb[:, 0:TP], data0=Rb[:, 0:TP], data1=Ib[:, 0:TP],
                                                                         initial=init, op0=ALU.mult, op1=ALU.add),
                       R=[bR[0], bR[1], bR[2], bI[0], bI[1], bI[2], bHALO], W=[bM[0], bM[1], bM[2]])
                    if not final:
                        op("dve", lambda h, f=f: h.tensor_copy(out=EXS[:, f, 3:4], in_=Mb[:, TP - 1:TP]),
                           R=[bM[2]], W=[bEXS])
                        continue
                    op("dve", lambda h, f=f: h.tensor_copy(out=OSH[:, f, 0:1], in_=Mb[:, TP - 1:TP]), R=[bM[2]], W=[bOSH])
                    op("dve", lambda h, f=f: h.tensor_tensor(out=Mb[:, TP:T], in0=Rb[:, TP:T], in1=STH[:, f, :], op=ALU.mult),
                       R=[bR[2], bSTH], W=[bM[2]])
                    op("dve", lambda h: h.tensor_tensor(out=Mb[:, TP:T], in0=Mb[:, TP:T], in1=Ib[:, TP:T], op=ALU.add),
                       R=[bM[2], bI[2]], W=[bM[2]])
                    op("act", lambda h, f=f: h.activation(out=OSH[:, f, 1:17], in_=Mb[:, TP:T], func=COPY), R=[bM[2]], W=[bOSH])
                    op("dve", lambda h, f=f: h.tensor_tensor(out=Z[:, f, 0:T], in0=Z[:, f, 0:T], in1=Mb[:, 0:T], op=ALU.mult),
                       R=bMa + bZ[f], W=bZ[f])

        Zf = Z[:].bitcast(F32)

        def zview(c0, c1):
            return Zf[:, c0:c1, :].rearrange("p a b -> p (a b)")
        SETS = [
            dict(UX=UX[:], U=[U[0][:], U[1][:]], UB=UB[:], R=Rb[:], I=Ib[:], M=Mb[:],
                 bUX=bUX, bU=bU, bUB=bUB, bR=bR, bI=bI, bM=bM),
            dict(UX=zview(0, 3), U=[zview(3, 5), zview(5, 7)], UB=Z[:, 13:15, :], R=zview(7, 9), I=zview(9, 11), M=zview(11, 13),
                 bUX=Buf(), bU=[[Buf() for _ in TILES] for _ in range(2)], bUB=[[Buf() for _ in TILES] for _ in range(2)],
                 bR=[Buf() for _ in TILES], bI=[Buf() for _ in TILES], bM=[Buf() for _ in TILES]),
        ]

        def rg_block1(l, n, S, s_ai, aoff):
            j = l // 2
            iw = V_RGW + 4 * j
            ib = V_RGB + j
            tiles = list(enumerate(PTILES))
            UXs, Us, UBs, Rs, Is, Ms = S["UX"], S["U"], S["UB"], S["R"], S["I"], S["M"]
            s_x = load_slab(wd("w_rgx")[j, n], 4096, avoid=s_ai)
            for cc in range(2):
                f = 2 * n + cc
                op("dve", lambda h: h.memset(UXs[:, 0:3], 0.0), W=[S["bUX"]])
                pbs = {}
                for ti, (t0, tn) in tiles:
                    pbs[ti] = bank()
                    mm_group(pbs[ti], tn, [RING[s_x][:, kc * 256 + cc * 128: kc * 256 + cc * 128 + 128] for kc in range(NCH)],
                             [HN[:, kc, t0:t0 + tn] for kc in range(NCH)], R=[bRING[s_x]] + [bHN[kc][ti] for kc in range(NCH)])
                yield
                for ti, (t0, tn) in tiles:
                    op("act", lambda h, t0=t0, tn=tn, pb=pbs[ti]: h.activation(out=UXs[:, 3 + t0:3 + t0 + tn], in_=PS[pb][:, 0:tn], func=COPY),
                       R=[bPS[pbs[ti]]], W=[S["bUX"]])
                yield
                bUall = list(S["bU"][cc])
                op("act", lambda h, cc=cc, f=f: h.activation(out=Us[cc][:, 0:TP], in_=UXs[:, 0:TP], func=COPY,
                                                              scale=vcol(iw, f), bias=vcol(ib, f)),
                   R=[S["bUX"], bVEC], W=bUall)
                yield
                for k in (1, 2, 3):
                    op("dve", lambda h, cc=cc, f=f, k=k: h.scalar_tensor_tensor(
                        out=Us[cc][:, 0:TP], in0=UXs[:, k:k + TP], scalar=vcol(iw + k, f), in1=Us[cc][:, 0:TP],
                        op0=ALU.mult, op1=ALU.add), R=[S["bUX"], bVEC] + bUall, W=bUall)
                yield
                op("act", lambda h, cc=cc: h.activation(out=UBs[:, cc, 0:TP], in_=Us[cc][:, 0:TP], func=COPY),
                   R=bUall, W=list(S["bUB"][cc]))
                op("dve", lambda h, f=f: h.tensor_copy(out=EXS[:, f, 0:3], in_=UXs[:, TP:TP + 3]), R=[S["bUX"]], W=[bEXS])
                yield
            for oc in range(2):
                f = 2 * n + oc
                prs, pis = {}, {}
                for ti, (t0, tn) in tiles:
                    prs[ti], pis[ti] = bank(), bank()
                    rhs = [UBs[:, kc, t0:t0 + tn] for kc in range(2)]
                    Ru = [bRING[s_ai], S["bUB"][0][ti], S["bUB"][1][ti]]
                    mm_group(prs[ti], tn, [RING[s_ai][:, aoff + kc * 256 + oc * 128: aoff + kc * 256 + oc * 128 + 128] for kc in range(2)], rhs, R=Ru)
                    mm_group(pis[ti], tn, [RING[s_ai][:, aoff + 512 + kc * 256 + oc * 128: aoff + 512 + kc * 256 + oc * 128 + 128] for kc in range(2)], rhs, R=Ru)
                    op("act", lambda h, t0=t0, tn=tn, pr=prs[ti], f=f: h.activation(out=Rs[:, t0:t0 + tn], in_=PS[pr][:, 0:tn], func=AF.Tanh,
                                                                                  scale=0.5, bias=HBV[:, j, 0, f:f + 1]),
                       R=[bPS[prs[ti]], bVEC], W=[S["bR"][ti]])
                    op("act", lambda h, t0=t0, tn=tn, pi=pis[ti], f=f: h.activation(out=Is[:, t0:t0 + tn], in_=PS[pi][:, 0:tn], func=AF.Tanh,
                                                                                  scale=0.5, bias=HBV[:, j, 1, f:f + 1]),
                       R=[bPS[pis[ti]], bVEC], W=[S["bI"][ti]])
                yield
                bRa, bIa, bMa = list(S["bR"]), list(S["bI"]), list(S["bM"])
                op("act", lambda h, f=f: h.activation(out=Rs[:, 0:TP], in_=Rs[:, 0:TP], func=AF.Exp,
                                                      scale=NLC[:, j, f:f + 1], bias=NLC[:, j, f:f + 1]),
                   R=bRa + [bNLC], W=bRa)
                op("act", lambda h: h.activation(out=Ms[:, 0:TP], in_=Rs[:, 0:TP], func=AF.Square), R=bRa, W=bMa)
                yield
                op("act", lambda h: h.activation(out=Ms[:, 0:TP], in_=Ms[:, 0:TP], func=AF.Sqrt, scale=-0.25, bias=0.25), R=bMa, W=bMa)
                yield
                op("dve", lambda h: h.scalar_tensor_tensor(out=Is[:, 0:TP], in0=Is[:, 0:TP], scalar=1.0, in1=Ms[:, 0:TP],
                                                           op0=ALU.add, op1=ALU.mult), R=bIa + bMa, W=bIa)
                op("dve", lambda h, oc=oc: h.tensor_tensor(out=Is[:, 0:TP], in0=Is[:, 0:TP], in1=Us[oc][:, 0:TP], op=ALU.mult),
                   R=bIa + list(S["bU"][oc]), W=bIa)
                yield
                op("dve", lambda h: h.tensor_tensor_scan(out=Ms[:, 0:TP], data0=Rs[:, 0:TP], data1=Is[:, 0:TP],
                                                         initial=0.0, op0=ALU.mult, op1=ALU.add),
                   R=S["bR"] + S["bI"], W=S["bM"])
                op("dve", lambda h, f=f: h.tensor_copy(out=EXS[:, f, 3:4], in_=Ms[:, TP - 1:TP]), R=[S["bM"][2]], W=[bEXS])
                yield

        def rg_pass1(l):
            j = l // 2
            for q in range(2):
                s_ai = state["slot"]
                state["slot"] = (s_ai + 1) % NSLOT
                dma("pool", ring_sem[s_ai], RING[s_ai][:, 0:4096].rearrange("p (b e) -> p b e", e=1024),
                    wd("w_rgai")[j, 4 * q:4 * q + 4].rearrange("b p e -> p b e"), W=[bRING[s_ai]])
                for pair in range(2):
                    n0 = 4 * q + 2 * pair
                    gens = [rg_block1(l, n0, SETS[0], s_ai, (n0 % 4) * 1024),
                            rg_block1(l, n0 + 1, SETS[1], s_ai, ((n0 + 1) % 4) * 1024)]
                    alive = [True, True]
                    while any(alive):
                        for gi in range(2):
                            if alive[gi]:
                                try:
                                    next(gens[gi])
                                except StopIteration:
                                    alive[gi] = False

        def rg_mixer(l):
            j = l // 2
            dma("sp", msem(), STR[:].rearrange("p a b c -> p (a b c)"), st_rgc[j], W=[bSTR])
            dma("sp", msem(), STH[:].rearrange("p a b -> p (a b)"), st_rgh[j], W=[bSTH])
            rg_pass1(l)
            pre = (load_slab(wd("w_rgx")[j, 0], 4096), load_slab(wd("w_rgai")[j, 0], 1024), load_slab(wd("w_rgg")[j, 0], 4096))
            exchange(l)
            rg_pass(l, True, pre=pre)
            so1, so2 = P.semsrc("or%d" % j), P.semsrc("oh%d" % j)
            out_sems.extend([so1, so2])
            dma("sp", so1, o_rgc[j], OSR[:].rearrange("p a b c -> p (a b c)"), R=[bOSR])
            dma("sp", so2, o_rgh[j], OSH[:].rearrange("p a b -> p (a b)"), R=[bOSH])
            out_proj(wd("w_rgout")[j], NCH, bZ, late=2)

        def ffn(l):
            for q in range(NQ):
                for jj in range(QCH):
                    jx = q * QCH + jj
                    s = load_slab(wd("w_gu")[l, jx], 4096)
                    for ti, (t0, tn) in enumerate(TILES):
                        pg, pu = bank(), bank()
                        hn = [HN[:, kc, t0:t0 + tn] for kc in range(NCH)]
                        Rh = [bRING[s]] + [bHN[kc][ti] for kc in range(NCH)]
                        mm_group(pg, tn, [RING[s][:, kc * 256: kc * 256 + 128] for kc in range(NCH)], hn, R=Rh)
                        mm_group(pu, tn, [RING[s][:, kc * 256 + 128: kc * 256 + 256] for kc in range(NCH)], hn, R=Rh)
                        gi = state["g"] % 2
                        state["g"] += 1
                        g_ = Gt[:, gi, 0:tn]
                        op("act", lambda h, g_=g_, pg=pg, tn=tn: h.activation(out=g_, in_=PS[pg][:, 0:tn], func=AF.Silu),
                           R=[bPS[pg]], W=[bG[gi]])
                        op("dve", lambda h, g_=g_, jj=jj, t0=t0, tn=tn, pu=pu: h.tensor_tensor(
                            out=Z[:, jj, t0:t0 + tn], in0=PS[pu][:, 0:tn], in1=g_, op=ALU.mult),
                            R=[bPS[pu], bG[gi]], W=[bZ[jj][ti]])
                out_proj(wd("w_dn")[l, q], QCH, bZ, late=1)

        def ple(l):
            sp_ = msem("pool")
            dma("pool", sp_, UB[:].rearrange("p a b -> p (a b)").rearrange("p (a b) -> p a b", b=T),
                pT[l].rearrange("p (a b) -> p a b", b=T), W=[b for row in bUB for b in row])
            for n in range(8):
                s = load_slab(wd("w_pg")[l, n], 4096)
                s2 = load_slab(wd("w_pp")[l, n], 512)
                for cc in range(2):
                    ch = 2 * n + cc
                    for ti, (t0, tn) in enumerate(TILES):
                        pg, pp_ = bank(), bank()
                        mm_group(pg, tn, [RING[s][:, kc * 256 + cc * 128: kc * 256 + cc * 128 + 128] for kc in range(NCH)],
                                 [HN[:, kc, t0:t0 + tn] for kc in range(NCH)],
                                 R=[bRING[s]] + [bHN[kc][ti] for kc in range(NCH)])
                        mm_group(pp_, tn, [RING[s2][:, kc * 256 + cc * 128: kc * 256 + cc * 128 + 128] for kc in range(2)],
                                 [UB[:, kc, t0:t0 + tn] for kc in range(2)],
                                 R=[bRING[s2], bUB[0][ti], bUB[1][ti]])
                        gi = state["g"] % 2
                        state["g"] += 1
                        g_ = Gt[:, gi, 0:tn]
                        op("act", lambda h, g_=g_, pg=pg, tn=tn: h.activation(out=g_, in_=PS[pg][:, 0:tn], func=AF.Sigmoid),
                           R=[bPS[pg]], W=[bG[gi]])
                        op("dve", lambda h, g_=g_, pp_=pp_, tn=tn: h.tensor_tensor(out=g_, in0=PS[pp_][:, 0:tn], in1=g_, op=ALU.mult),
                           R=[bPS[pp_], bG[gi]], W=[bG[gi]])
                        op("dve", lambda h, g_=g_, ch=ch, t0=t0, tn=tn: h.tensor_tensor(
                            out=X[:, ch, t0:t0 + tn], in0=X[:, ch, t0:t0 + tn], in1=g_, op=ALU.add),
                            R=[bG[gi]], W=[bX[ch][ti]])

        phases = []
        for l in range(4):
            phases.append(lambda l=l: norm(V_MIX + l))
            phases.append((lambda l=l: conv_mixer(l)) if l % 2 == 0 else (lambda l=l: rg_mixer(l)))
            phases.append(lambda l=l: norm(V_FFN + l))
            phases.append(lambda l=l: ffn(l))
            phases.append(lambda l=l: norm(V_PLE + l))
            phases.append(lambda l=l: ple(l))
        phases.append(lambda: norm(V_FIN, dst="X"))
        for ph in (phases if stop is None else phases[:stop]):
            ph()
        y3 = y_d.rearrange("p (a b) -> p a b", b=T)
        for ti, (t0, tn) in enumerate(TILES):
            sy = P.semsrc("yout%d" % ti)
            out_sems.append(sy)
            dma("sp", sy, y3[:, :, t0:t0 + tn], X[:, :, t0:t0 + tn], R=[bX[c][ti] for c in range(NCH)])
        P.wait_all("sp", out_sems)

        with nc.Block() as block:
            @block.tensor
            def _(h):
                for f_ in P.E["pe"].prog:
                    f_(h)

            @block.scalar
            def _(h):
                for f_ in P.E["act"].prog:
                    f_(h)

            @block.vector
            def _(h):
                for f_ in P.E["dve"].prog:
                    f_(h)

            @block.gpsimd
            def _(h):
                for f_ in P.E["pool"].prog:
                    f_(h)

            @block.sync
            def _(h):
                for f_ in P.E["sp"].prog:
                    f_(h)
    return nc


def _slab(W, kc):
    n = W.shape[1]
    return W.reshape(kc, 128, n).transpose(1, 0, 2).reshape(128, kc * n)


def _feat_major(a):
    t, d = a.shape
    return a.T.reshape(d // 128, 128, t).transpose(1, 0, 2)


def _cols256(W):
    K = W.shape[0]
    kc = K // 128
    return np.ascontiguousarray(W.reshape(kc, 128, 8, 256).transpose(2, 1, 0, 3)).reshape(8, 128, kc * 256)


def prepare(x_prompt, x_sample, p_prompt, p_sample, state_conv, state_rg_conv, state_rg_h,
            mix_norm, ffn_norm, ple_norm, final_norm,
            sc_w_in, sc_w_conv, sc_w_out,
            rg_w_x, rg_w_gate, rg_conv_w, rg_conv_b, rg_w_a, rg_b_a, rg_w_i, rg_b_i, rg_lambda, rg_w_out,
            ffn_w_gate, ffn_w_up, ffn_w_down, ple_w_gate, ple_w_proj):
    f32 = np.float32
    A = lambda a: np.asarray(a, dtype=f32)
    x_prompt, x_sample, p_prompt, p_sample = A(x_prompt), A(x_sample), A(p_prompt), A(p_sample)
    state_conv, state_rg_conv, state_rg_h = A(state_conv), A(state_rg_conv), A(state_rg_h)

    vl = [A(mix_norm)[l] for l in range(4)] + [A(ffn_norm)[l] for l in range(4)] + [A(ple_norm)[l] for l in range(4)]
    vl += [A(final_norm)]
    vl += [A(sc_w_conv)[j, k] for j in range(2) for k in range(3)]
    vl += [A(rg_conv_w)[j, k] for j in range(2) for k in range(4)]
    vl += [A(rg_conv_b)[j] for j in range(2)] + [A(rg_b_a)[j] for j in range(2)] + [A(rg_b_i)[j] for j in range(2)]
    vl += [A(rg_lambda)[j] for j in range(2)]
    assert len(vl) == NV
    vecs = np.ascontiguousarray(np.stack(vl, 0).reshape(NV, NCH, 128).transpose(2, 0, 1)).reshape(128, NV * NCH)

    w_in = A(sc_w_in)
    Wc = w_in[:, :, 2048:4096].reshape(2, 16, 128, 16, 128)
    Wx = w_in[:, :, 4096:6144].reshape(2, 16, 128, 16, 128)
    w_sccx = np.ascontiguousarray(np.stack([Wc, Wx], axis=4).transpose(0, 3, 2, 1, 4, 5)).reshape(2, 16, 128, 4096)
    w_scb = np.stack([_cols256(w_in[j, :, 0:2048]) for j in range(2)])
    w_scout = np.stack([_cols256(A(sc_w_out)[j]) for j in range(2)])
    w_rgx = np.stack([_cols256(A(rg_w_x)[j]) for j in range(2)])
    w_rgg = np.stack([_cols256(A(rg_w_gate)[j]) for j in range(2)])
    w_rgout = np.stack([_cols256(A(rg_w_out)[j]) for j in range(2)])
    wa = A(rg_w_a).reshape(2, 8, 2, 128, 256).transpose(0, 1, 3, 2, 4).reshape(2, 8, 128, 512)
    wi = A(rg_w_i).reshape(2, 8, 2, 128, 256).transpose(0, 1, 3, 2, 4).reshape(2, 8, 128, 512)
    w_rgai = np.ascontiguousarray(np.concatenate([wa, wi], axis=3))
    Wg = A(ffn_w_gate).reshape(4, 16, 128, 44, 128)
    Wu = A(ffn_w_up).reshape(4, 16, 128, 44, 128)
    w_gu = np.ascontiguousarray(np.stack([Wg, Wu], axis=4).transpose(0, 3, 2, 1, 4, 5)).reshape(4, 44, 128, 4096)
    w_dn = np.ascontiguousarray(A(ffn_w_down).reshape(4, NQ, QCH, 128, 8, 256).transpose(0, 1, 4, 3, 2, 5)).reshape(
        4, NQ, 8, 128, QCH * 256)
    w_pg = np.stack([_cols256(A(ple_w_gate)[l]) for l in range(4)])
    w_pp = np.stack([_cols256(A(ple_w_proj)[l]) for l in range(4)])

    shared = dict(vecs=vecs, w_sccx=w_sccx, w_scb=w_scb, w_scout=w_scout, w_rgx=w_rgx, w_rgg=w_rgg,
                  w_rgai=w_rgai, w_rgout=w_rgout, w_gu=w_gu, w_dn=w_dn, w_pg=w_pg, w_pp=w_pp)

    in_maps = []
    for c in range(8):
        s, hf = c // 2, c % 2
        xt = np.concatenate([x_prompt[s, hf * TP:(hf + 1) * TP], x_sample[c * NS:(c + 1) * NS, 0]], axis=0)
        xT = np.ascontiguousarray(_feat_major(xt)).reshape(128, NCH * T)
        pt = np.concatenate([p_prompt[:, s, hf * TP:(hf + 1) * TP], p_sample[:, c * NS:(c + 1) * NS, 0]], axis=1)
        pTt = np.ascontiguousarray(pt.transpose(0, 2, 1).reshape(4, 2, 128, T).transpose(0, 2, 1, 3)).reshape(4, 128, 2 * T)
        sc = state_conv[:, c * NS:(c + 1) * NS]
        stc = np.ascontiguousarray(sc.reshape(2, NS, 2, NCH, 128).transpose(0, 4, 3, 2, 1)).reshape(2, 128, NCH * 2 * NS)
        sr = state_rg_conv[:, c * NS:(c + 1) * NS]
        strc = np.ascontiguousarray(sr.reshape(2, NS, 3, NCH, 128).transpose(0, 4, 3, 2, 1)).reshape(2, 128, NCH * 3 * NS)
        sh = state_rg_h[:, c * NS:(c + 1) * NS]
        sth = np.ascontiguousarray(sh.reshape(2, NS, NCH, 128).transpose(0, 3, 2, 1)).reshape(2, 128, NCH * NS)
        m = dict(shared)
        m.update(xT=xT, pT=pTt, st_conv=stc, st_rgc=strc, st_rgh=sth,
                 flag=np.full((128, 1), float(hf), f32))
        in_maps.append(m)
    return in_maps


def kernel(**inputs):
    in_maps = prepare(**inputs)
    nc = build_program()
    res = run_bass_kernel_spmd(nc, in_maps, core_ids=list(range(8)))
    return assemble(res.results)


def assemble(R, ncores=8):
    f32 = np.float32

    y_prompt = np.empty((4, 2048, D), f32)
    y_sample = np.empty((128, 1, D), f32)
    conv_p = np.empty((2, 4, 2, D), f32)
    conv_s = np.empty((2, 128, 2, D), f32)
    rgc_p = np.empty((2, 4, 3, D), f32)
    rgc_s = np.empty((2, 128, 3, D), f32)
    rgh_p = np.empty((2, 4, D), f32)
    rgh_s = np.empty((2, 128, D), f32)
    for c in range(ncores):
        s, hf = c // 2, c % 2
        y = np.asarray(R[c]["y"]).reshape(128, NCH, T).transpose(2, 1, 0).reshape(T, D)
        y_prompt[s, hf * TP:(hf + 1) * TP] = y[:TP]
        y_sample[c * NS:(c + 1) * NS, 0] = y[TP:]
        oc = np.asarray(R[c]["o_conv"]).reshape(2, 128, NCH, 2, 17).transpose(0, 4, 3, 2, 1).reshape(2, 17, 2, D)
        orc = np.asarray(R[c]["o_rgc"]).reshape(2, 128, NCH, 3, 17).transpose(0, 4, 3, 2, 1).reshape(2, 17, 3, D)
        oh = np.asarray(R[c]["o_rgh"]).reshape(2, 128, NCH, 17).transpose(0, 3, 2, 1).reshape(2, 17, D)
        conv_s[:, c * NS:(c + 1) * NS] = oc[:, 1:]
        rgc_s[:, c * NS:(c + 1) * NS] = orc[:, 1:]
        rgh_s[:, c * NS:(c + 1) * NS] = oh[:, 1:]
        if hf == 1:
            conv_p[:, s] = oc[:, 0]
            rgc_p[:, s] = orc[:, 0]
            rgh_p[:, s] = oh[:, 0]
    return (y_prompt, y_sample, conv_p, conv_s, rgc_p, rgc_s, rgh_p, rgh_s)
```
